# Optimizing a Trainium2 kernel written in Bass

```python
import math
import jax, jax.numpy as jnp
from jax import lax
import numpy as np

D_MODEL = 1024
BATCH = 32
SEQ = 2048
DEPTH = 1

N_META = 16
NORM_EPS = 1e-6
HG_HEADS = 8
HG_DK = D_MODEL // HG_HEADS
HG_DV = D_MODEL // HG_HEADS
HG_WIDTH = HG_HEADS * HG_DK
HG_CHUNK = 16
MLA_HEADS = 8
QK_NOPE = 128
QK_ROPE = 64
V_HEAD = 128
Q_LORA = 256
KV_LORA = 256
ROPE_THETA = 10000.0
ATTN_BLOCK = 128
FFN_HIDDEN = ((8 * D_MODEL + 3 * 256 - 1) // (3 * 256)) * 256
IN_SIZES = (HG_WIDTH, HG_WIDTH, HG_WIDTH, HG_WIDTH,
            Q_LORA, KV_LORA, QK_ROPE,
            D_MODEL, D_MODEL)
IN_COLS = sum(IN_SIZES)

kernel_name = "hybrid_hgrn2_mla_gated_block"


def rms_norm(x, g):
    xf = x.astype(jnp.float32)
    y = xf * lax.rsqrt(jnp.mean(xf * xf, axis=-1, keepdims=True) + NORM_EPS)
    return (y * g.astype(jnp.float32)).astype(x.dtype)


def rope_tables(length):
    pos = jnp.arange(length, dtype=jnp.float32)
    inv_freq = 1.0 / (ROPE_THETA ** (jnp.arange(0, QK_ROPE, 2, dtype=jnp.float32) / QK_ROPE))
    ang = pos[:, None] * inv_freq[None, :]
    return jnp.cos(ang), jnp.sin(ang)


def apply_rope(t, cos, sin):
    t32 = t.astype(jnp.float32)
    t1, t2 = jnp.split(t32, 2, axis=-1)
    return jnp.concatenate([t1 * cos - t2 * sin, t2 * cos + t1 * sin], axis=-1).astype(t.dtype)


def hgrn2_chunk_scan(q, k, v, logf):
    B, L, H, _ = q.shape
    n = L // HG_CHUNK

    def to_chunks(t):
        return t.reshape(B, n, HG_CHUNK, H, t.shape[-1]).transpose(1, 0, 3, 2, 4)

    xs = (to_chunks(q), to_chunks(k), to_chunks(v), to_chunks(logf))
    causal = jnp.tril(jnp.ones((HG_CHUNK, HG_CHUNK), dtype=bool))[:, :, None]

    def step(S, inp):
        qb, kb, vb, gb = inp
        b = jnp.cumsum(gb, axis=2)
        o_inter = jnp.einsum('bhtk,bhkv->bhtv', qb * jnp.exp(b), S)
        diff = b[:, :, :, None, :] - b[:, :, None, :, :]
        decay = jnp.exp(jnp.where(causal, diff, -jnp.inf))
        A = jnp.einsum('bhtsk,bhsk->bhts', decay * qb[:, :, :, None, :], kb)
        o_intra = jnp.einsum('bhts,bhsv->bhtv', A, vb)
        b_last = b[:, :, -1:, :]
        S_new = jnp.exp(b_last[:, :, 0, :])[..., None] * S + jnp.einsum(
            'bhsk,bhsv->bhkv', kb * jnp.exp(b_last - b), vb)
        return S_new, o_inter + o_intra

    S0 = jnp.zeros((B, H, q.shape[-1], v.shape[-1]), jnp.float32)
    _, ys = lax.scan(step, S0, xs)
    return ys.transpose(1, 0, 3, 2, 4).reshape(B, L, H, v.shape[-1])


def hgrn2_mixer(q, f_pre, i, g, lb, norm_g):
    B, L, _ = q.shape
    split = lambda t: t.reshape(B, L, HG_HEADS, -1).astype(jnp.float32)
    qh = jax.nn.silu(split(q))
    lbh = lb.astype(jnp.float32).reshape(HG_HEADS, HG_DK)
    fgate = lbh + (1.0 - lbh) * jax.nn.sigmoid(split(f_pre))
    o = hgrn2_chunk_scan(qh, 1.0 - fgate, split(i), jnp.log(fgate))
    o = rms_norm(o, norm_g) * jax.nn.silu(split(g))
    return o.reshape(B, L, HG_WIDTH).astype(q.dtype)


def mla_mixer(c_q, c_kv, k_pe, q_norm_g, w_q_b, kv_norm_g, w_kv_b, cos, sin):
    B, L, _ = c_q.shape
    q = (rms_norm(c_q, q_norm_g) @ w_q_b).reshape(B, L, MLA_HEADS, QK_NOPE + QK_ROPE)
    q_nope = q[..., :QK_NOPE]
    q_pe = apply_rope(q[..., QK_NOPE:], cos[:, None, :], sin[:, None, :])
    kv = (rms_norm(c_kv, kv_norm_g) @ w_kv_b).reshape(B, L, MLA_HEADS, QK_NOPE + V_HEAD)
    k_nope, v = kv[..., :QK_NOPE], kv[..., QK_NOPE:]
    k_pe = apply_rope(k_pe, cos, sin)
    scale = (QK_NOPE + QK_ROPE) ** -0.5
    bounds = [(0, N_META)] + [(N_META + s, min(N_META + s + ATTN_BLOCK, L))
                              for s in range(0, L - N_META, ATTN_BLOCK)]
    outs = []
    for start, end in bounds:
        s = (jnp.einsum('bqhd,bkhd->bhqk', q_nope[:, start:end], k_nope[:, :end])
             + jnp.einsum('bqhr,bkr->bhqk', q_pe[:, start:end], k_pe[:, :end]))
        s = s.astype(jnp.float32) * scale
        qpos = jnp.arange(start, end)[:, None]
        kpos = jnp.arange(end)[None, :]
        s = jnp.where(kpos <= qpos, s, -jnp.inf)
        p = jax.nn.softmax(s, axis=-1).astype(v.dtype)
        outs.append(jnp.einsum('bhqk,bkhd->bqhd', p, v[:, :end]))
    o = jnp.concatenate(outs, axis=1)
    return o.reshape(B, L, MLA_HEADS * V_HEAD)


def setup_inputs(seed: int = 0) -> dict:
    key = jax.random.key(seed)
    ks = jax.random.split(key, 20)
    nrm = lambda k, shape, scale: jax.random.normal(k, shape, jnp.float32) * scale
    gain = lambda k, shape: 1.0 + 0.02 * jax.random.normal(k, shape, jnp.float32)
    return {
        "x": nrm(ks[0], (BATCH, SEQ, D_MODEL), 1.0),
        "meta_tokens": nrm(ks[1], (N_META, D_MODEL), 1.0),
        "w_in": nrm(ks[2], (DEPTH, D_MODEL, IN_COLS), D_MODEL ** -0.5),
        "b_gate": nrm(ks[3], (DEPTH, 2 * D_MODEL), 0.01),
        "lb_logits": nrm(ks[4], (DEPTH + 1, HG_WIDTH), 0.1),
        "hg_norm_g": gain(ks[5], (DEPTH, HG_DV)),
        "w_hg_o": nrm(ks[6], (DEPTH, HG_WIDTH, D_MODEL), HG_WIDTH ** -0.5),
        "q_a_norm_g": gain(ks[7], (DEPTH, Q_LORA)),
        "w_q_b": nrm(ks[8], (DEPTH, Q_LORA, MLA_HEADS * (QK_NOPE + QK_ROPE)), Q_LORA ** -0.5),
        "kv_a_norm_g": gain(ks[9], (DEPTH, KV_LORA)),
        "w_kv_b": nrm(ks[10], (DEPTH, KV_LORA, MLA_HEADS * (QK_NOPE + V_HEAD)), KV_LORA ** -0.5),
        "w_mla_o": nrm(ks[11], (DEPTH, MLA_HEADS * V_HEAD, D_MODEL), (MLA_HEADS * V_HEAD) ** -0.5),
        "w_out": nrm(ks[12], (DEPTH, D_MODEL, D_MODEL), D_MODEL ** -0.5),
        "mix_pre_g": gain(ks[13], (DEPTH, D_MODEL)),
        "mix_post_g": gain(ks[14], (DEPTH, D_MODEL)),
        "ffn_pre_g": gain(ks[15], (DEPTH, D_MODEL)),
        "ffn_post_g": gain(ks[16], (DEPTH, D_MODEL)),
        "w_ffn_in": nrm(ks[17], (DEPTH, D_MODEL, 2 * FFN_HIDDEN), D_MODEL ** -0.5),
        "w_ffn_out": nrm(ks[18], (DEPTH, FFN_HIDDEN, D_MODEL), FFN_HIDDEN ** -0.5),
    }


def reference(x, meta_tokens, w_in, b_gate, lb_logits, hg_norm_g, w_hg_o, q_a_norm_g, w_q_b,
              kv_a_norm_g, w_kv_b, w_mla_o, w_out, mix_pre_g, mix_post_g, ffn_pre_g, ffn_post_g,
              w_ffn_in, w_ffn_out):
    B = x.shape[0]
    meta = jnp.broadcast_to(meta_tokens[None].astype(x.dtype), (B, N_META, D_MODEL))
    h = jnp.concatenate([meta, x], axis=1)
    L = h.shape[1]
    cos, sin = rope_tables(L)
    lower_bounds = jnp.cumsum(jax.nn.softmax(lb_logits.astype(jnp.float32), axis=0), axis=0)
    splits = []
    acc = 0
    for sz in IN_SIZES[:-2]:
        acc += sz
        splits.append(acc)
    for l in range(DEPTH):
        u = rms_norm(h, mix_pre_g[l])
        proj = u @ w_in[l]
        hq, hf, hi, hg, cq, ckv, kpe, gates = jnp.split(proj, splits, axis=-1)
        y_a = hgrn2_mixer(hq, hf, hi, hg, lower_bounds[l], hg_norm_g[l]) @ w_hg_o[l]
        y_b = mla_mixer(cq, ckv, kpe, q_a_norm_g[l], w_q_b[l], kv_a_norm_g[l], w_kv_b[l],
                        cos, sin) @ w_mla_o[l]
        gate_a, gate_b = jnp.split(jax.nn.sigmoid(gates + b_gate[l]), 2, axis=-1)
        mixed = (gate_a * y_a + gate_b * y_b) @ w_out[l]
        h = h + rms_norm(mixed, mix_post_g[l])
        u = rms_norm(h, ffn_pre_g[l])
        gt, up = jnp.split(u @ w_ffn_in[l], 2, axis=-1)
        h = h + rms_norm((jax.nn.silu(gt) * up) @ w_ffn_out[l], ffn_post_g[l])
    return h[:, N_META:, :]
```

```python
import math
from contextlib import ExitStack
import numpy as np
import concourse.bass as bass
import concourse.mybir as mybir
from concourse.bass_utils import run_bass_kernel_spmd

F32 = mybir.dt.float32
BF16 = mybir.dt.bfloat16
AF = mybir.ActivationFunctionType
ALU = mybir.AluOpType

NCORES = 8
BATCH = 32
SEQ = 2048
D = 1024
NSEQ = BATCH // NCORES
NT = 2
T = NT * 128
NSUP = SEQ // T
NH = 8
EPS = 1e-6
FFN_H = 2816
NKC_F = FFN_H // 128
NBLK = 1 + SEQ // 128
SCALE = (128 + 64) ** -0.5
NSLOT = 4
CH = 4096

CH_H = [("H", h) for h in range(NH)]
CH_L = [("L1", 0), ("L2", 0)]
CH_KV = [("WKV", 0)]
CH_Q = [("WQ", 0)]
CH_MX = [("MX", f) for f in range(8)]
CH_WO = [("WO", hf) for hf in range(2)]
CH_FI = [("FI", j) for j in range(11)]
FO_GROUPS = [(0, 8), (8, 16), (16, 22)]
CH_FO = [("FO", hf * 3 + g) for hf in range(2) for g in range(3)]
CHUNKS = CH_H + CH_L + CH_KV + CH_Q + CH_MX + CH_WO + CH_FI + CH_FO
CHIDX = {c: i for i, c in enumerate(CHUNKS)}
NCH = len(CHUNKS)
SEQ_FULL = [CHIDX[c] for c in CHUNKS]
SEQ_META = [CHIDX[c] for c in CH_H + CH_L + CH_KV]

NCF = 64


def _pack_weights(w_in, w_hg_o, w_q_b, w_kv_b, w_mla_o, w_out, w_ffn_in, w_ffn_out):
    wp = np.zeros((NCH, 128, CH), np.float32)

    def put(ci, mat):
        K, nco = mat.shape
        kc = K // 128
        wp[ci, :, : kc * nco] = mat.reshape(kc, 128, nco).transpose(1, 0, 2).reshape(128, kc * nco)

    sw = np.concatenate([np.arange(32, 64), np.arange(0, 32)])
    for h in range(NH):
        cols = np.concatenate([np.arange(h * 128, (h + 1) * 128) + off for off in (0, 1024, 3072, 2048)])
        put(CHIDX[("H", h)], w_in[:, cols])
    put(CHIDX[("L1", 0)], w_in[:, 4096:4608])
    kpe_cols = np.arange(4608, 4672)
    put(CHIDX[("L2", 0)], w_in[:, np.concatenate([kpe_cols, kpe_cols[sw]])])
    cols = []
    for h in range(NH):
        base = h * 192
        cols += [np.arange(base, base + 128), np.arange(base + 128, base + 192), (np.arange(base + 128, base + 192))[sw]]
    put(CHIDX[("WQ", 0)], w_q_b[:, np.concatenate(cols)])
    kcols = np.concatenate([np.arange(h * 256, h * 256 + 128) for h in range(NH)])
    vcols = np.concatenate([np.arange(h * 256 + 128, h * 256 + 256) for h in range(NH)])
    put(CHIDX[("WKV", 0)], w_kv_b[:, np.concatenate([kcols, vcols])])
    for f in range(8):
        sl = slice(f * 128, (f + 1) * 128)
        m = np.concatenate([w_hg_o[:, sl], w_mla_o[:, sl], w_in[:, 4672 + f * 128: 4672 + (f + 1) * 128],
                            w_in[:, 5696 + f * 128: 5696 + (f + 1) * 128]], axis=1)
        put(CHIDX[("MX", f)], m)
    for hf in range(2):
        put(CHIDX[("WO", hf)], w_out[:, hf * 512:(hf + 1) * 512])
    for j in range(11):
        a, b = 2 * j, 2 * j + 1
        m = np.concatenate([w_ffn_in[:, a * 128:(a + 1) * 128], w_ffn_in[:, FFN_H + a * 128: FFN_H + (a + 1) * 128],
                            w_ffn_in[:, b * 128:(b + 1) * 128], w_ffn_in[:, FFN_H + b * 128: FFN_H + (b + 1) * 128]], axis=1)
        put(CHIDX[("FI", j)], m)
    for hf in range(2):
        for g, (k0, k1) in enumerate(FO_GROUPS):
            put(CHIDX[("FO", hf * 3 + g)], w_ffn_out[k0 * 128:k1 * 128, hf * 512:(hf + 1) * 512])
    return wp


def _rope_tables():
    pos = np.concatenate([np.zeros(112, np.float32), np.arange(16, dtype=np.float32),
                          np.arange(16, 16 + SEQ, dtype=np.float32)])
    inv = (1.0 / (np.float32(10000.0) ** (np.arange(0, 64, 2, dtype=np.float32) / np.float32(64)))).astype(np.float32)
    ang = (pos[None, :] * inv[:, None]).astype(np.float32)
    c, s = np.cos(ang).astype(np.float32), np.sin(ang).astype(np.float32)
    tab = np.zeros((64, 2, pos.shape[0]), np.float32)
    tab[:32, 0], tab[32:, 0] = c, c
    tab[:32, 1], tab[32:, 1] = -s, s
    return tab


def _const_mats():
    m = np.zeros((128, 384), np.float32)
    m[:, 0:128] = np.eye(128, dtype=np.float32)
    m[:, 128:256] = np.triu(np.ones((128, 128), np.float32))
    m[112:, 256:384] = 1.0
    return m


class Eng:
    def __init__(self, name, is_pe=False):
        self.name, self.is_pe = name, is_pe
        self.ops, self.sem, self.cnt, self.seen = [], None, 0, {}


class Buf:
    __slots__ = ("name", "w", "r", "excl")

    def __init__(self, name, excl=False):
        self.name, self.w, self.r, self.excl = name, None, {}, excl


class DSem:
    def __init__(self, sem):
        self.sem, self.val = sem, 0


class Prog:
    def __init__(self):
        self.pe, self.act, self.dve = Eng("pe", True), Eng("act"), Eng("dve")
        self.pool, self.sp = Eng("pool"), Eng("sp")

    def op(self, eng, fn, reads=(), writes=(), dsem=None):
        waits = {}

        def need(tok):
            if tok is None:
                return
            sem, val = tok
            if eng.is_pe and sem is eng.sem:
                return
            if eng.seen.get(sem, 0) >= val:
                return
            if waits.get(sem, 0) < val:
                waits[sem] = val

        writes = list({id(b): b for b in list(writes) + [b for b in reads if b.excl]}.values())
        reads = list({id(b): b for b in reads if not b.excl}.values())
        for b in reads:
            need(b.w)
        for b in writes:
            need(b.w)
            for t in b.r.items():
                need(t)
        for sem, val in waits.items():
            eng.seen[sem] = val
        if dsem is not None:
            dsem.val += 16
            tok, inc = (dsem.sem, dsem.val), 16
        else:
            eng.cnt += 1
            tok, inc = (eng.sem, eng.cnt), 1
        for b in reads:
            if b.r.get(tok[0], 0) < tok[1]:
                b.r[tok[0]] = tok[1]
        for b in writes:
            b.w, b.r = tok, {}
        eng.ops.append((list(waits.items()), fn, tok[0], inc))
        return tok

    def wait_only(self, eng, toks):
        waits = {}
        for sem, val in toks:
            if eng.seen.get(sem, 0) < val and waits.get(sem, 0) < val:
                waits[sem] = val
        for sem, val in waits.items():
            eng.seen[sem] = val
        eng.ops.append((list(waits.items()), None, None, 0))


def _emit(eng, e):
    for waits, fn, sem, inc in eng.ops:
        for s, v in waits:
            e.wait_ge(s, v)
        if fn is not None:
            fn(e).then_inc(sem, inc)


def build_program(nseq=NSEQ, nsup=NSUP):
    order = _build(nseq, nsup, None)
    return _build(nseq, nsup, order)


def _build(nseq, nsup, use_order):
    nc = bass.Bass("TRN2", target_bir_lowering=False)
    xs = nc.dram_tensor("xs", [NSEQ, SEQ, D], F32, kind="ExternalInput").ap()
    xmeta = nc.dram_tensor("xmeta", [128, D], F32, kind="ExternalInput").ap()
    wpack = nc.dram_tensor("wpack", [NCH, 128, CH], F32, kind="ExternalInput").ap()
    cf32d = nc.dram_tensor("cf32", [128, NCF], F32, kind="ExternalInput").ap()
    gpostd = nc.dram_tensor("gpost", [128, 2 * D], F32, kind="ExternalInput").ap()
    roped = nc.dram_tensor("rope", [64, 2, 128 + SEQ], F32, kind="ExternalInput").ap()
    cmatd = nc.dram_tensor("cmat", [128, 384], F32, kind="ExternalInput").ap()
    outd = nc.dram_tensor("out", [NSEQ, SEQ, D], F32, kind="ExternalOutput").ap()
    wbf = nc.dram_tensor("wbf", [NCH, 128, CH], BF16, kind="Internal").ap()

    P = Prog()
    with ExitStack() as es:
        def sb(name, shape, dt):
            return es.enter_context(nc.sbuf_tensor(name, shape, dt))

        def sem(name):
            return es.enter_context(nc.semaphore(name))

        for e in (P.pe, P.act, P.dve, P.pool, P.sp):
            e.sem = sem("s_" + e.name)

        ring = [sb(f"ring{i}", [128, CH], BF16) for i in range(NSLOT)]
        ringB = [Buf(f"ring{i}") for i in range(NSLOT)]
        ringS = [DSem(sem(f"ringsem{i}")) for i in range(NSLOT)]
        Kn = sb("Kn", [128, NH, NBLK * 128], BF16)
        Kpe = sb("Kpe", [128, NBLK * 128], BF16)
        Vc = sb("Vc", [128, NBLK, D], BF16)
        KnB = [Buf(f"Kn{b}") for b in range(NBLK)]
        KpeB = [Buf(f"Kpe{b}") for b in range(NBLK)]
        VB = [Buf(f"V{b}") for b in range(NBLK)]
        hb = sb("h", [128, 2, NT, D], F32)
        hB = [[Buf(f"h{p}_{t}") for t in range(NT)] for p in range(2)]
        xS = [DSem(sem(f"xsem{p}")) for p in range(2)]
        stS = [[DSem(sem(f"stsem{p}_{t}")) for t in range(NT)] for p in range(2)]
        ropeb = sb("ropeb", [64, 2, 2, T], F32)
        ropeB = [Buf(f"rope{p}") for p in range(2)]
        rS = [DSem(sem(f"rsem{p}")) for p in range(2)]
        uTm = sb("uTm", [128, 8, T], BF16)
        uTmB = [Buf(f"uTm{t}") for t in range(NT)]
        uTf = sb("uTf", [128, 8, T], BF16)
        uTfB = [Buf(f"uTf{t}") for t in range(NT)]
        uT, uTB = uTm, uTmB
        xn2 = sb("xn2", [128, NT, D], BF16)
        xn2B = [Buf(f"xn2_{t}") for t in range(NT)]
        small = sb("small", [128, 64], F32)
        smallB = {}

        def sm(name):
            if name not in smallB:
                smallB[name] = (len(smallB), Buf("sm_" + name))
            i, b = smallB[name]
            return small[:, i:i + 1], b

        NFT = 12
        ft = [sb(f"ft{i}", [128, T], F32) for i in range(NFT)]
        ftB = [Buf(f"ft{i}") for i in range(NFT)]
        NBT = 9
        bt = [sb(f"bt{i}", [128, T], BF16) for i in range(NBT)]
        btB = [Buf(f"bt{i}") for i in range(NBT)]
        tmpM = sb("tmpM", [128, 128], F32)
        tmpMB = Buf("tmpM")
        Sbf = sb("Sbf", [128, 128], BF16)
        SbfB = Buf("Sbf")
        S = sb("S", [128, NH, 128], F32)
        SB = [Buf(f"S{h}") for h in range(NH)]
        Smeta = sb("Smeta", [128, NH, 128], F32)
        SmetaB = Buf("Smeta")
        big = sb("big", [128, 24, T], BF16)
        bigB = [Buf(f"big{i}") for i in range(24)]
        qn = sb("qn", [128, NH, T], BF16)
        qnB = [Buf(f"qn{h}") for h in range(NH)]
        qpe = sb("qpe", [128, NH, T], BF16)
        qpeB = [Buf(f"qpe{h}") for h in range(NH)]
        cqn = sb("cqn", [128, 2, T], BF16)
        cqnB = Buf("cqn")
        ckvn = sb("ckvn", [128, 2, T], BF16)
        ckvnB = Buf("ckvn")
        PT = [sb(f"PT{i}", [128, 2 * T], BF16) for i in range(4)]
        PTB = [Buf(f"PT{i}") for i in range(4)]
        y0 = sb("y0", [128, NT, 512], F32)
        y0B = [Buf(f"y0_{t}") for t in range(NT)]
        gpostb = sb("gpostb", [128, 2 * D], F32)
        cf = sb("cf_sb", [128, NCF], F32)
        cd = sb("cd", [128, 32], F32)
        cmat = sb("cmat_sb", [128, 384], BF16)
        onesf = sb("onesf", [128, 128], F32)
        dummy = sb("dummy_sb", [128, 8], F32)

        constB = Buf("const")
        cS = DSem(sem("csem"))
        ident, triu, ones_meta = cmat[:, 0:128], cmat[:, 128:256], cmat[:, 256:384]

        psF = [es.enter_context(nc.psum_tensor(f"psF{i}", [128, 512], F32)) for i in range(7)]
        psF.append(es.enter_context(nc.psum_tensor("psF7", [128, 512], F32)))
        psBt = psF[7][:, :].bitcast(BF16)
        U = [psF[i // 2][:, (i % 2) * 256:(i % 2) * 256 + 256] for i in range(14)]
        bankB = [Buf(f"bank{i}", excl=True) for i in range(8)]
        UB = [bankB[i // 2] for i in range(14)]
        PB = [psBt[:, q * 256:(q + 1) * 256] for q in range(4)]
        PBB = [bankB[7] for q in range(4)]
        pb_rr = [0]

        def next_pb():
            q = pb_rr[0] % 4
            pb_rr[0] += 1
            return PB[q], PBB[q]

        P.op(P.sp, lambda e: e.dma_start(out=cf[:], in_=cf32d), writes=[constB], dsem=cS)
        P.op(P.sp, lambda e: e.dma_start(out=gpostb[:], in_=gpostd), writes=[constB], dsem=cS)
        cS2 = DSem(sem("csem2"))
        cmB = Buf("cmatB")
        P.op(P.pool, lambda e: e.dma_start(out=cmat[:], in_=cmatd), writes=[cmB], dsem=cS2)
        P.op(P.pool, lambda e: e.memset(onesf[:], 1.0), reads=[cmB], writes=[constB])
        P.op(P.pool, lambda e: e.memset(S[:], 0.0), writes=SB)
        P.op(P.pool, lambda e: e.memset(Kpe[64:128, :], 0.0), writes=KpeB)
        P.op(P.pool, lambda e: e.memset(qpe[64:128, :, :], 0.0), writes=qpeB)
        P.op(P.dve, lambda e: e.tensor_tensor(out=cd[:, 0:8], in0=cf[:, 16:24], in1=cf[:, 24:32], op=ALU.subtract),
             reads=[constB], writes=[constB])
        P.op(P.act, lambda e: e.activation(out=cd[:, 0:8], in_=cd[:, 0:8], func=AF.Sigmoid), reads=[constB], writes=[constB])
        P.op(P.dve, lambda e: e.tensor_scalar(out=cd[:, 8:16], in0=cd[:, 0:8], scalar1=-1.0, scalar2=1.0,
                                              op0=ALU.mult, op1=ALU.add), reads=[constB], writes=[constB])
        lb_ap = lambda h: cd[:, h:h + 1]
        oml_ap = lambda h: cd[:, 8 + h:9 + h]

        NWG = 13
        wS = [DSem(sem(f"wsem{i}")) for i in range(NWG)]
        wbfB = [Buf(f"wbf{c}") for c in range(NCH)]
        for c in range(NCH):
            g = c // 3
            P.op(P.pool, (lambda c: lambda e: e.dma_start(out=wbf[c], in_=wpack[c]))(c), writes=[], dsem=wS[g])
        for c in range(NCH):
            g = c // 3
            wbfB[c].w = (wS[g].sem, wS[g].val)

        use_seq = []
        state = {"use": 0, "load": 0}

        def issue_load(i):
            c = use_seq[i]
            s = i % NSLOT
            P.op(P.sp, (lambda c, s: lambda e: e.dma_start(out=ring[s][:], in_=wbf[c]))(c, s),
                 reads=[wbfB[c]], writes=[ringB[s]], dsem=ringS[s])

        recorded = []
        held = set()

        def try_loads(k):
            while state["load"] < min(len(use_seq), k + NSLOT) and (state["load"] - NSLOT) not in held:
                issue_load(state["load"])
                state["load"] += 1

        def wget(expect, hold=False):
            k = state["use"]
            state["use"] += 1
            state["lastk"] = k
            recorded.append(CHIDX[expect])
            if hold:
                held.add(k)
            if use_order is not None:
                assert use_seq[k] == CHIDX[expect], (k, use_seq[k], expect)
                try_loads(k)
            s = k % NSLOT
            return ring[s], ringB[s]

        def release(k):
            held.discard(k)
            if use_order is not None:
                try_loads(state["use"] - 1)

        def mm_chain(out_ap, pairs):
            def fn(e):
                ins = None
                n = len(pairs)
                for i, (l, r) in enumerate(pairs):
                    ins = e.matmul(out_ap, lhsT=l, rhs=r, start=(i == 0), stop=(i == n - 1))
                return ins
            return fn

        def rstd_from(ss_ap, ss_bufs, n, out_ap, out_buf, width_bufs=()):
            P.op(P.act, lambda e: e.activation(out=out_ap, in_=ss_ap, func=AF.Ln, scale=1.0 / n, bias=EPS),
                 reads=ss_bufs, writes=[out_buf])
            P.op(P.act, lambda e: e.activation(out=out_ap, in_=out_ap, func=AF.Exp, scale=-0.5),
                 reads=[out_buf], writes=[out_buf])

        def norm_stats(par, t):
            hx = hb[:, par, t, :]
            ss_ap, ssB = sm(f"nt_ss{t}")
            rs_ap, rsB = sm(f"nt_rs{t}")
            P.op(P.act, lambda e: e.activation(out=xn2[:, t, :], in_=hx, func=AF.Square, accum_out=ss_ap),
                 reads=[hB[par][t]], writes=[xn2B[t], ssB])
            yield
            rstd_from(ss_ap, [ssB], D, rs_ap, rsB)
            yield
            P.op(P.act, lambda e: e.activation(out=xn2[:, t, :], in_=hx, func=AF.Copy, scale=rs_ap),
                 reads=[hB[par][t], rsB], writes=[xn2B[t]])
            yield

        def norm_tr(t, gcol0, uTx, uTxB):
            pb, pb2 = PB[2 * (t % 2)], PB[2 * (t % 2) + 1]
            pbB = pbB2 = PBB[0]
            for g4 in range(2):
                def fn(e, g4=g4):
                    ins = None
                    for i in range(4):
                        kc = g4 * 4 + i
                        dst = (pb if i < 2 else pb2)[:, (i % 2) * 128:(i % 2) * 128 + 128]
                        ins = e.transpose(dst, xn2[:, t, kc * 128:(kc + 1) * 128], ident)
                    return ins
                P.op(P.pe, fn, reads=[xn2B[t], constB], writes=[pbB])
                yield
                for half, pp in enumerate((pb, pb2)):
                    kc0 = g4 * 4 + half * 2
                    P.op(P.dve, (lambda pp, kc0: lambda e: e.tensor_tensor(
                        out=uTx[:, kc0:kc0 + 2, t * 128:(t + 1) * 128],
                        in0=pp.rearrange("p (c t) -> p c t", t=128),
                        in1=cf[:, gcol0 + kc0:gcol0 + kc0 + 2].unsqueeze(2).to_broadcast([128, 2, 128]),
                        op=ALU.mult))(pp, kc0), reads=[pbB, constB], writes=[uTxB[t]])
                    yield

        def chain(*gens):
            for g in gens:
                yield from g

        def rr_merge(gens):
            gens = list(gens)
            while gens:
                for g in list(gens):
                    try:
                        next(g)
                        yield
                    except StopIteration:
                        gens.remove(g)

        def hgrn_A(h, ntl, meta, hi, cx):
            Tn = ntl * 128
            pp = hi % 2
            _, emidB = sm(f"emid{pp}_0")
            for c in range(1, NT):
                sm(f"emid{pp}_{c}")
            _, elB = sm(f"elast{pp}_0")
            for c in range(1, NT):
                sm(f"elast{pp}_{c}")
            ei = smallB[f"emid{pp}_0"][0]
            li = smallB[f"elast{pp}_0"][0]
            cx.update(h=h, ntl=ntl, meta=meta, pp=pp, ei=ei, li=li, emidB=emidB, elB=elB)
            w, wB = wget(("H", h), hold=True)
            kk = state["lastk"]
            wv = w[:].rearrange("p (k c) -> p k c", c=512)
            base = 0 if pp == 0 else 8
            uq, uf, ug, uv = (base + i for i in range(4))
            uTr = uTB[:ntl]
            if not meta:
                P.op(P.pe, mm_chain(U[uq][:, :Tn], [(wv[:, k, 0:128], uT[:, k, :Tn]) for k in range(8)]),
                     reads=[wB] + uTr, writes=[UB[uq]])
                yield
            P.op(P.pe, mm_chain(U[uf][:, :Tn], [(wv[:, k, 128:256], uT[:, k, :Tn]) for k in range(8)]),
                 reads=[wB] + uTr, writes=[UB[uf]])
            yield
            if not meta:
                P.op(P.pe, mm_chain(U[ug][:, :Tn], [(wv[:, k, 256:384], uT[:, k, :Tn]) for k in range(8)]),
                     reads=[wB] + uTr, writes=[UB[ug]])
                yield
            for t in range(ntl):
                P.op(P.pe, mm_chain(U[uv][:, t * 128:(t + 1) * 128],
                                    [(uT[:, k, t * 128:(t + 1) * 128], wv[:, k, 384:512]) for k in range(8)]),
                     reads=[wB, uTB[t]], writes=[UB[uv]])
                if t == ntl - 1:
                    release(kk)
                yield

        def hgrn_B(cx):
            h, ntl, meta, pp, ei, li, emidB, elB = (cx[k] for k in ("h", "ntl", "meta", "pp", "ei", "li", "emidB", "elB"))
            Tn = ntl * 128
            base = 0 if pp == 0 else 8
            uq, uf, ug, uv = (base + i for i in range(4))
            QH, FG, LF, KK, BM, EI = 0, 1, 2, 3, 4, 7
            E_, GS = 5 + pp, 8 + pp
            QD, KD, VBt = 0 + pp, 2 + pp, 4 + pp
            f = lambda i: ft[i][:, :Tn]
            b_ = lambda i: bt[i][:, :Tn]
            def sig3(dst, dstB, src, srcB):
                P.op(P.act, lambda e: e.activation(out=f(dst), in_=src, func=AF.Exp, scale=-1.0), reads=[srcB], writes=[ftB[dst]])
                yield
                P.op(P.act, lambda e: e.activation(out=f(dst), in_=f(dst), func=AF.Ln, scale=1.0, bias=1.0), reads=[ftB[dst]], writes=[ftB[dst]])
                yield
                P.op(P.act, lambda e: e.activation(out=f(dst), in_=f(dst), func=AF.Exp, scale=-1.0), reads=[ftB[dst]], writes=[ftB[dst]])
                yield
            yield from sig3(FG, None, U[uf][:, :Tn], UB[uf])
            if not meta:
                yield from sig3(QH, None, U[uq][:, :Tn], UB[uq])
                yield from sig3(GS, None, U[ug][:, :Tn], UB[ug])
                P.op(P.dve, lambda e: e.tensor_tensor(out=f(QH), in0=U[uq][:, :Tn], in1=f(QH), op=ALU.mult),
                     reads=[UB[uq], ftB[QH]], writes=[ftB[QH]])
                yield
            P.op(P.dve, lambda e: e.tensor_scalar(out=f(FG), in0=f(FG), scalar1=oml_ap(h), scalar2=lb_ap(h),
                                                  op0=ALU.mult, op1=ALU.add), reads=[ftB[FG], constB], writes=[ftB[FG]])
            yield
            P.op(P.act, lambda e: e.activation(out=f(LF), in_=f(FG), func=AF.Ln), reads=[ftB[FG]], writes=[ftB[LF]])
            yield
            P.op(P.pool, lambda e: e.tensor_scalar(out=f(KK), in0=f(FG), scalar1=-1.0, scalar2=1.0, op0=ALU.mult, op1=ALU.add),
                 reads=[ftB[FG]], writes=[ftB[KK]])
            yield

            def scan_fn(e):
                ins = None
                for c in range(ntl):
                    cs = slice(c * 128, (c + 1) * 128)
                    ins = e.tensor_tensor_scan(out=ft[LF][:, cs], data0=onesf[:], data1=ft[LF][:, cs], initial=0.0,
                                               op0=ALU.mult, op1=ALU.add)
                return ins
            P.op(P.dve, scan_fn, reads=[ftB[LF], constB], writes=[ftB[LF]])
            yield
            lfv = ft[LF][:, :Tn].rearrange("p (c t) -> p c t", t=128)
            P.op(P.dve, lambda e: e.tensor_tensor(out=ft[BM][:, :Tn].rearrange("p (c t) -> p c t", t=128), in0=lfv,
                                                  in1=lfv[:, :, 63:64].to_broadcast([128, ntl, 128]), op=ALU.subtract),
                 reads=[ftB[LF]], writes=[ftB[BM]])
            yield
            P.op(P.act, lambda e: e.activation(out=f(E_), in_=f(BM), func=AF.Exp), reads=[ftB[BM]], writes=[ftB[E_]])
            yield
            P.op(P.act, lambda e: e.activation(out=f(EI), in_=f(BM), func=AF.Exp, scale=-1.0), reads=[ftB[BM]], writes=[ftB[EI]])
            yield
            emid = small[:, ei:ei + ntl]
            elast = small[:, li:li + ntl]
            P.op(P.act, lambda e: e.activation(out=emid.unsqueeze(2), in_=lfv[:, :, 63:64], func=AF.Exp),
                 reads=[ftB[LF]], writes=[emidB])
            yield
            if not meta:
                P.op(P.dve, lambda e: e.tensor_tensor(out=f(GS), in0=U[ug][:, :Tn], in1=f(GS), op=ALU.mult),
                     reads=[UB[ug], ftB[GS]], writes=[ftB[GS]])
                yield
            P.op(P.dve, lambda e: e.tensor_copy(out=b_(VBt), in_=U[uv][:, :Tn]), reads=[UB[uv]], writes=[btB[VBt]])
            yield
            Ev = ft[E_][:, :Tn].rearrange("p (c t) -> p c t", t=128)
            P.op(P.dve, lambda e: e.tensor_tensor(out=elast.unsqueeze(2), in0=emid.unsqueeze(2), in1=Ev[:, :, 127:128], op=ALU.mult),
                 reads=[emidB, ftB[E_]], writes=[elB])
            yield
            if not meta:
                P.op(P.dve, lambda e: e.tensor_tensor(out=b_(QD), in0=f(QH), in1=f(E_), op=ALU.mult),
                     reads=[ftB[QH], ftB[E_]], writes=[btB[QD]])
                yield
            P.op(P.pool, lambda e: e.tensor_tensor(out=b_(KD), in0=f(KK), in1=f(EI), op=ALU.mult),
                 reads=[ftB[KK], ftB[EI]], writes=[btB[KD]])
            yield

        def hgrn_C(cx, a_gen=None):
            def ains():
                if a_gen is not None:
                    next(a_gen, None)
            h, ntl, meta, pp, ei, li, emidB, elB = (cx[k] for k in ("h", "ntl", "meta", "pp", "ei", "li", "emidB", "elB"))
            Tn = ntl * 128
            E_, GS = 5 + pp, 8 + pp
            QD, KD, VBt = 0 + pp, 2 + pp, 4 + pp
            KDT, AM, SQ = 6, 7, 8
            T1, RS = 10, 11
            f = lambda i: ft[i][:, :Tn]
            b_ = lambda i: bt[i][:, :Tn]
            pb, pbB = next_pb()

            def tr_fn(e):
                ins = None
                for c in range(ntl):
                    ins = e.transpose(pb[:, c * 128:(c + 1) * 128], bt[KD][:, c * 128:(c + 1) * 128], ident)
                return ins
            P.op(P.pe, tr_fn, reads=[btB[KD], constB], writes=[pbB])
            yield
            P.op(P.dve, lambda e: e.tensor_copy(out=b_(KDT), in_=pb[:, :Tn]), reads=[pbB], writes=[btB[KDT]])
            yield
            UA, UO, USS, UM = 4, 5, 6, 7
            if meta:
                ains()
            if not meta:
                def at_fn(e):
                    ins = None
                    for c in range(ntl):
                        cs = slice(c * 128, (c + 1) * 128)
                        ins = e.matmul(U[UA][:, cs], lhsT=bt[KD][:, cs], rhs=bt[QD][:, cs], start=True, stop=True)
                    return ins
                P.op(P.pe, at_fn, reads=[btB[KD], btB[QD]], writes=[UB[UA]])
                ains()
                yield
                P.op(P.dve, lambda e: e.tensor_tensor(out=bt[AM][:, :Tn].rearrange("p (c t) -> p c t", t=128),
                                                      in0=U[UA][:, :Tn].rearrange("p (c t) -> p c t", t=128),
                                                      in1=triu.unsqueeze(1).to_broadcast([128, ntl, 128]), op=ALU.mult),
                     reads=[UB[UA], constB], writes=[btB[AM]])
                yield
            for c in range(ntl):
                cs = slice(c * 128, (c + 1) * 128)
                if not meta:
                    P.op(P.dve, (lambda c: lambda e: e.tensor_scalar(out=Sbf[:], in0=S[:, h, :], scalar1=small[:, ei + c:ei + c + 1],
                                                                       scalar2=None, op0=ALU.mult))(c),
                         reads=[SB[h], emidB], writes=[SbfB])
                    yield
                    P.op(P.pe, (lambda cs: lambda e: (e.matmul(U[UO][:, cs], lhsT=bt[VBt][:, cs], rhs=bt[AM][:, cs], start=True, stop=False),
                                                       e.matmul(U[UO][:, cs], lhsT=Sbf[:], rhs=bt[QD][:, cs], start=False, stop=True))[1])(cs),
                         reads=[btB[VBt], btB[AM], SbfB, btB[QD]], writes=[UB[UO]])
                    yield
                P.op(P.pe, (lambda c, cs: lambda e: e.matmul(U[UM][:, 0:128], lhsT=bt[KDT][:, cs], rhs=bt[VBt][:, cs], start=True, stop=True))(c, cs),
                     reads=[btB[KDT], btB[VBt]], writes=[UB[UM]])
                ains()
                yield
                P.op(P.act, (lambda c: lambda e: e.activation(out=tmpM[:], in_=U[UM][:, 0:128], func=AF.Copy,
                                                              scale=ft[E_][:, c * 128 + 127:c * 128 + 128]))(c),
                     reads=[UB[UM], ftB[E_]], writes=[tmpMB])
                yield
                P.op(P.dve, (lambda c: lambda e: e.scalar_tensor_tensor(out=S[:, h, :], in0=S[:, h, :], scalar=small[:, li + c:li + c + 1],
                                                                         in1=tmpM[:], op0=ALU.mult, op1=ALU.add))(c),
                     reads=[SB[h], elB, tmpMB], writes=[SB[h]])
                yield
            if meta:
                while a_gen is not None and next(a_gen, "done") != "done":
                    pass
                return
            P.op(P.act, lambda e: e.activation(out=b_(SQ), in_=U[UO][:, :Tn], func=AF.Square), reads=[UB[UO]], writes=[btB[SQ]])
            ains()
            yield
            P.op(P.pe, lambda e: e.matmul(U[USS][:, :Tn], lhsT=cmat_ones(), rhs=b_(SQ), start=True, stop=True),
                 reads=[btB[SQ], constB], writes=[UB[USS]])
            while a_gen is not None and next(a_gen, "done") != "done":
                pass
            yield
            rstd_from(U[USS][:, :Tn], [UB[USS]], 128, f(RS), ftB[RS])
            yield
            P.op(P.dve, lambda e: e.scalar_tensor_tensor(out=f(T1), in0=U[UO][:, :Tn], scalar=cf[:, 32:33], in1=f(RS),
                                                          op0=ALU.mult, op1=ALU.mult),
                 reads=[UB[UO], ftB[RS], constB], writes=[ftB[T1]])
            yield
            P.op(P.pool, lambda e: e.tensor_tensor(out=big[:, h, :Tn], in0=f(T1), in1=f(GS), op=ALU.mult),
                 reads=[ftB[T1], ftB[GS]], writes=[bigB[h]])
            yield

        onesb = sb("onesb", [128, 128], BF16)
        P.op(P.pool, lambda e: e.memset(onesb[:], 1.0), writes=[constB])
        cmat_ones = lambda: onesb[:]

        def rope_apply(ps_a, ps_b, bufs_in, rpar, Tn, out_ap, out_bufs, fa=8):
            F_A, F_B = fa, fa + 1
            P.op(P.dve, lambda e: e.tensor_tensor(out=ft[F_A][0:64, :Tn], in0=ps_a, in1=ropeb[:, rpar, 0, :Tn], op=ALU.mult),
                 reads=bufs_in[0:1] + [ropeB[rpar]], writes=[ftB[F_A]])
            P.op(P.dve, lambda e: e.tensor_tensor(out=ft[F_B][0:64, :Tn], in0=ps_b, in1=ropeb[:, rpar, 1, :Tn], op=ALU.mult),
                 reads=bufs_in[1:2] + [ropeB[rpar]], writes=[ftB[F_B]])
            P.op(P.pool, lambda e: e.tensor_tensor(out=out_ap, in0=ft[F_A][0:64, :Tn], in1=ft[F_B][0:64, :Tn], op=ALU.add),
                 reads=[ftB[F_A], ftB[F_B]], writes=out_bufs)

        def latent_norm(ps0, ps1, psB_, ussi, gcol, out_t, out_B, Tn):
            SQ0, SQ1 = 0, 2
            RS = 7
            P.op(P.act, lambda e: e.activation(out=bt[SQ0][:, :Tn], in_=ps0, func=AF.Square), reads=[psB_[0]], writes=[btB[SQ0]])
            yield
            P.op(P.act, lambda e: e.activation(out=bt[SQ1][:, :Tn], in_=ps1, func=AF.Square), reads=[psB_[1]], writes=[btB[SQ1]])
            yield
            P.op(P.pe, mm_chain(U[ussi][:, :Tn], [(onesb[:], bt[SQ0][:, :Tn]), (onesb[:], bt[SQ1][:, :Tn])]),
                 reads=[btB[SQ0], btB[SQ1], constB], writes=[UB[ussi]])
            yield
            rstd_from(U[ussi][:, :Tn], [UB[ussi]], 256, ft[RS][:, :Tn], ftB[RS])
            yield
            for j, ps in enumerate((ps0, ps1)):
                P.op(P.dve, (lambda j, ps: lambda e: e.scalar_tensor_tensor(out=out_t[:, j, :Tn], in0=ps, scalar=cf[:, gcol + j:gcol + j + 1],
                                                                              in1=ft[RS][:, :Tn], op0=ALU.mult, op1=ALU.mult))(j, ps),
                     reads=[psB_[j], ftB[RS], constB], writes=[out_B])
                yield

        def mla_latents(ntl, meta, rpar, blk0):
            Tn = ntl * 128
            w, wB = wget(("L1", 0))
            wv = w[:].rearrange("p (k c) -> p k c", c=512)
            uTr = uTB[:ntl]
            for i in range(4):
                if meta and i < 2:
                    continue
                P.op(P.pe, mm_chain(U[i][:, :Tn], [(wv[:, k, i * 128:(i + 1) * 128], uT[:, k, :Tn]) for k in range(8)]),
                     reads=[wB] + uTr, writes=[UB[i]])
                yield
            if not meta:
                yield from latent_norm(U[0][:, :Tn], U[1][:, :Tn], [UB[0], UB[1]], 8, 33, cqn, cqnB, Tn)
            yield from latent_norm(U[2][:, :Tn], U[3][:, :Tn], [UB[2], UB[3]], 9, 35, ckvn, ckvnB, Tn)
            w2, w2B = wget(("L2", 0))
            w2v = w2[:, 0:8 * 128].rearrange("p (k c) -> p k c", c=128)
            P.op(P.pe, mm_chain(U[10][0:64, :Tn], [(w2v[:, k, 0:64], uT[:, k, :Tn]) for k in range(8)]),
                 reads=[w2B] + uTr, writes=[UB[10]])
            yield
            P.op(P.pe, mm_chain(U[11][0:64, :Tn], [(w2v[:, k, 64:128], uT[:, k, :Tn]) for k in range(8)]),
                 reads=[w2B] + uTr, writes=[UB[11]])
            yield
            rope_apply(U[10][0:64, :Tn], U[11][0:64, :Tn], [UB[10], UB[11]], rpar, Tn,
                       Kpe[0:64, blk0 * 128: blk0 * 128 + Tn], [KpeB[blk0 + t] for t in range(ntl)], fa=0)
            yield

        def kv_gen(ntl, blk0):
            Tn = ntl * 128
            w, wB = wget(("WKV", 0))
            wv = w[:].rearrange("p (k c) -> p k c", c=2048)
            for h in range(NH):
                u = (8, 10, 12)[h % 3]
                P.op(P.pe, mm_chain(U[u][:, :Tn], [(wv[:, j, h * 128:(h + 1) * 128], ckvn[:, j, :Tn]) for j in range(2)]),
                     reads=[wB, ckvnB], writes=[UB[u]])
                eng = P.act if h % 2 == 0 else P.dve
                if h % 2 == 0:
                    fn = (lambda h, u: lambda e: e.activation(out=Kn[:, h, blk0 * 128: blk0 * 128 + Tn], in_=U[u][:, :Tn], func=AF.Copy))(h, u)
                else:
                    fn = (lambda h, u: lambda e: e.tensor_copy(out=Kn[:, h, blk0 * 128: blk0 * 128 + Tn], in_=U[u][:, :Tn]))(h, u)
                P.op(eng, fn, reads=[UB[u]], writes=[KnB[blk0 + t] for t in range(ntl)])
            i = 0
            for t in range(ntl):
                for hf in range(2):
                    bank = i % 2
                    i += 1
                    full = psF[bank][:, :]
                    P.op(P.pe, mm_chain(full, [(ckvn[:, j, t * 128:(t + 1) * 128], wv[:, j, 1024 + hf * 512: 1024 + (hf + 1) * 512])
                                               for j in range(2)]),
                         reads=[wB, ckvnB], writes=[UB[2 * bank], UB[2 * bank + 1]])
                    if hf == 0:
                        fn = (lambda t, hf, full: lambda e: e.activation(out=Vc[:, blk0 + t, hf * 512:(hf + 1) * 512], in_=full, func=AF.Copy))(t, hf, full)
                        P.op(P.act, fn, reads=[UB[2 * bank], UB[2 * bank + 1]], writes=[VB[blk0 + t]])
                    else:
                        fn = (lambda t, hf, full: lambda e: e.tensor_copy(out=Vc[:, blk0 + t, hf * 512:(hf + 1) * 512], in_=full))(t, hf, full)
                        P.op(P.dve, fn, reads=[UB[2 * bank], UB[2 * bank + 1]], writes=[VB[blk0 + t]])

        def attention(j, rpar, blk0):
            w, wB = wget(("WQ", 0))
            wv = w[:].rearrange("p (k c) -> p k c", c=2048)
            nkb = blk0 + NT
            for h in range(NH):
                un, up_, us_ = (8, 10, 11) if h % 2 == 0 else (4, 6, 7)
                P.op(P.pe, mm_chain(U[un][:, :], [(wv[:, jj, h * 256: h * 256 + 128], cqn[:, jj, :]) for jj in range(2)]),
                     reads=[wB, cqnB], writes=[UB[un]])
                P.op(P.pe, mm_chain(U[up_][0:64, :], [(wv[:, jj, h * 256 + 128: h * 256 + 192], cqn[:, jj, :]) for jj in range(2)]),
                     reads=[wB, cqnB], writes=[UB[up_]])
                P.op(P.pe, mm_chain(U[us_][0:64, :], [(wv[:, jj, h * 256 + 192: h * 256 + 256], cqn[:, jj, :]) for jj in range(2)]),
                     reads=[wB, cqnB], writes=[UB[us_]])
                P.op(P.act, (lambda un, h: lambda e: e.activation(out=qn[:, h, :], in_=U[un][:, :], func=AF.Copy))(un, h),
                     reads=[UB[un]], writes=[qnB[h]])
                rope_apply(U[up_][0:64, :], U[us_][0:64, :], [UB[up_], UB[us_]], rpar, T, qpe[0:64, h, :], [qpeB[h]], fa=8 + h % 2 * 2)
            rr = {"pt0": 0, "pt1": 0}
            c0_of = lambda kb: (kb - blk0) * 128 if kb >= blk0 else 0
            groups = []
            kb = 0
            while kb < nkb:
                if kb + 1 < nkb and c0_of(kb) == 0 and c0_of(kb + 1) == 0:
                    groups.append([kb, kb + 1])
                    kb += 2
                else:
                    groups.append([kb])
                    kb += 1

            def head_gen(h, uo, ud, sbanks):
                hp = h % 2

                def s_op(gi):
                    bank = sbanks[gi % 2]
                    blks = groups[gi]

                    def fn(e):
                        ins = None
                        for i, kb in enumerate(blks):
                            c0 = c0_of(kb)
                            dst = psF[bank][:, i * T + c0:(i + 1) * T]
                            e.matmul(dst, lhsT=Kn[:, h, kb * 128:(kb + 1) * 128], rhs=qn[:, h, c0:], start=True, stop=False)
                            ins = e.matmul(dst, lhsT=Kpe[:, kb * 128:(kb + 1) * 128], rhs=qpe[:, h, c0:], start=False, stop=True)
                        return ins
                    P.op(P.pe, fn, reads=[KnB[kb] for kb in blks] + [KpeB[kb] for kb in blks] + [qnB[h], qpeB[h]], writes=[bankB[bank]])

                s_op(0)
                yield
                for gi, blks in enumerate(groups):
                    if gi + 1 < len(groups):
                        s_op(gi + 1)
                        yield
                    bank = sbanks[gi % 2]
                    pi = 2 * hp + rr["pt" + str(hp)] % 2
                    rr["pt" + str(hp)] += 1
                    lo = c0_of(blks[0])
                    hi_ = len(blks) * T
                    P.op(P.act, (lambda bank, pi, lo, hi_: lambda e: e.activation(out=PT[pi][:, lo:hi_], in_=psF[bank][:, lo:hi_], func=AF.Exp, scale=SCALE))(bank, pi, lo, hi_),
                         reads=[bankB[bank]], writes=[PTB[pi]])
                    yield
                    for i, kb in enumerate(blks):
                        if kb >= blk0:
                            o0 = i * T + c0_of(kb)
                            P.op(P.pool, (lambda o0, pi: lambda e: e.tensor_tensor(out=PT[pi][:, o0:o0 + 128], in0=PT[pi][:, o0:o0 + 128],
                                                                                    in1=triu, op=ALU.mult))(o0, pi),
                                 reads=[PTB[pi], constB], writes=[PTB[pi]])
                            yield

                    def pv_fn(e, blks=blks, pi=pi):
                        ins = None
                        for i, kb in enumerate(blks):
                            c0 = c0_of(kb)
                            first, last = kb == 0, kb == nkb - 1
                            onesl = ones_meta if kb == 0 else onesb[:]
                            rhs = PT[pi][:, i * T + c0:(i + 1) * T]
                            e.matmul(U[uo][:, c0:], lhsT=Vc[:, kb, h * 128:(h + 1) * 128], rhs=rhs, start=first, stop=last)
                            ins = e.matmul(U[ud][:, c0:], lhsT=onesl, rhs=rhs, start=first, stop=last)
                        return ins
                    P.op(P.pe, pv_fn, reads=[VB[kb] for kb in blks] + [PTB[pi], constB], writes=[UB[uo], UB[ud]])
                    yield
                RD = 6 + hp
                P.op(P.dve, lambda e: e.reciprocal(out=ft[RD][:, :], in_=U[ud][:, :]), reads=[UB[ud]], writes=[ftB[RD]])
                yield
                P.op(P.dve, lambda e: e.tensor_tensor(out=big[:, 8 + h, :], in0=U[uo][:, :], in1=ft[RD][:, :], op=ALU.mult),
                     reads=[UB[uo], ftB[RD]], writes=[bigB[8 + h]])
                yield

            for h in range(0, NH, 2):
                run_interleaved([head_gen(h, 4, 6, (0, 1)), head_gen(h + 1, 8, 10, (6, 7))])

        def mix_stage():
            for fc in range(8):
                w, wB = wget(("MX", fc))
                wv = w[:].rearrange("p (k c) -> p k c", c=512)
                b0 = 0 if fc % 2 == 0 else 4
                pa, pb_, pga, pgb = b0, b0 + 1, b0 + 2, b0 + 3
                P.op(P.pe, mm_chain(U[pa][:, :], [(wv[:, k, 0:128], big[:, k, :]) for k in range(8)]),
                     reads=[wB] + bigB[0:8], writes=[UB[pa]])
                P.op(P.pe, mm_chain(U[pb_][:, :], [(wv[:, k, 128:256], big[:, 8 + k, :]) for k in range(8)]),
                     reads=[wB] + bigB[8:16], writes=[UB[pb_]])
                P.op(P.pe, mm_chain(U[pga][:, :], [(wv[:, k, 256:384], uT[:, k, :]) for k in range(8)]),
                     reads=[wB] + uTB, writes=[UB[pga]])
                P.op(P.pe, mm_chain(U[pgb][:, :], [(wv[:, k, 384:512], uT[:, k, :]) for k in range(8)]),
                     reads=[wB] + uTB, writes=[UB[pgb]])
                GA, GB, TA, TB = 0, 1, 2, 3
                P.op(P.act, (lambda fc, pga: lambda e: e.activation(out=ft[GA][:, :], in_=U[pga][:, :], func=AF.Sigmoid, bias=cf[:, 37 + fc:38 + fc]))(fc, pga),
                     reads=[UB[pga], constB], writes=[ftB[GA]])
                P.op(P.act, (lambda fc, pgb: lambda e: e.activation(out=ft[GB][:, :], in_=U[pgb][:, :], func=AF.Sigmoid, bias=cf[:, 45 + fc:46 + fc]))(fc, pgb),
                     reads=[UB[pgb], constB], writes=[ftB[GB]])
                P.op(P.dve, (lambda pa: lambda e: e.tensor_tensor(out=ft[TA][:, :], in0=U[pa][:, :], in1=ft[GA][:, :], op=ALU.mult))(pa),
                     reads=[UB[pa], ftB[GA]], writes=[ftB[TA]])
                P.op(P.dve, (lambda pb_: lambda e: e.tensor_tensor(out=ft[TB][:, :], in0=U[pb_][:, :], in1=ft[GB][:, :], op=ALU.mult))(pb_),
                     reads=[UB[pb_], ftB[GB]], writes=[ftB[TB]])
                P.op(P.pool, (lambda fc: lambda e: e.tensor_tensor(out=big[:, 16 + fc, :], in0=ft[TA][:, :], in1=ft[TB][:, :], op=ALU.add))(fc),
                     reads=[ftB[TA], ftB[TB]], writes=[bigB[16 + fc]])

        def out_proj(par, chunks, in_off, gcol, final_store, seq_i, tok0, after_tile=None):
            accs = [(psF[4 + t][:, :], [UB[8 + 2 * t], UB[9 + 2 * t]]) for t in range(NT)]
            ssq = [[sm(f"op_ss{t}_{hf}") for hf in range(2)] for t in range(NT)]
            for hf in range(2):
                ngroups = len(chunks[hf])
                for gi, (cname, k0, k1) in enumerate(chunks[hf]):
                    w, wB = wget(cname, hold=True)
                    kk = state["lastk"]
                    wv = w[:, 0:(k1 - k0) * 512].rearrange("p (k c) -> p k c", c=512)
                    for t in range(NT):
                        acc, accB = accs[t]

                        def fn(e, t=t, acc=acc, k0=k0, k1=k1, wv=wv, gi=gi, ngroups=ngroups):
                            ins = None
                            for k in range(k0, k1):
                                ins = e.matmul(acc, lhsT=big[:, in_off + k, t * 128:(t + 1) * 128], rhs=wv[:, k - k0, :],
                                               start=(gi == 0 and k == k0), stop=(gi == ngroups - 1 and k == k1 - 1))
                            return ins
                        P.op(P.pe, fn, reads=[wB] + bigB[in_off + k0: in_off + k1], writes=accB)
                        if t == NT - 1:
                            release(kk)
                        yield
                for t in range(NT):
                    acc, accB = accs[t]
                    ss_ap, ssB = ssq[t][hf]
                    P.op(P.act, (lambda acc, ss_ap, t, hf: lambda e: e.activation(out=xn2[:, t, hf * 512:(hf + 1) * 512], in_=acc, func=AF.Square, accum_out=ss_ap))(acc, ss_ap, t, hf),
                         reads=accB, writes=[xn2B[t], ssB])
                    yield
                    if hf == 0:
                        P.op(P.dve, (lambda t, acc: lambda e: e.tensor_copy(out=y0[:, t, :], in_=acc))(t, acc), reads=accB, writes=[y0B[t]])
                        yield
            def epi(t):
                acc, accB = accs[t]
                (s0, s0B), (s1, s1B) = ssq[t]
                rs_ap, rsB = sm(f"op_rs{t}")
                P.op(P.dve, lambda e: e.tensor_tensor(out=s0, in0=s0, in1=s1, op=ALU.add), reads=[s0B, s1B], writes=[s0B])
                yield
                rstd_from(s0, [s0B], D, rs_ap, rsB)
                yield
                for hf in range(2):
                    src = y0[:, t, :] if hf == 0 else acc
                    srcB = [y0B[t]] if hf == 0 else accB
                    P.op(P.dve, (lambda src, hf: lambda e: e.scalar_tensor_tensor(out=src, in0=src, scalar=rs_ap,
                                                                                  in1=gpostb[:, gcol + hf * 512: gcol + (hf + 1) * 512],
                                                                                  op0=ALU.mult, op1=ALU.mult))(src, hf),
                         reads=srcB + [rsB, constB], writes=srcB)
                    yield
                    P.op(P.dve, (lambda src, hf: lambda e: e.tensor_tensor(out=hb[:, par, t, hf * 512:(hf + 1) * 512],
                                                                           in0=hb[:, par, t, hf * 512:(hf + 1) * 512], in1=src, op=ALU.add))(src, hf),
                         reads=srcB + [hB[par][t]], writes=[hB[par][t]])
                    yield
                if after_tile is not None:
                    yield from after_tile(t)
                if final_store:
                    P.op(P.pool, lambda e: e.dma_start(out=outd[seq_i, tok0 + t * 128: tok0 + (t + 1) * 128, :], in_=hb[:, par, t, :]),
                         reads=[hB[par][t]], writes=[], dsem=stS[par][t])
                    yield
            yield from rr_merge([epi(t) for t in range(NT)])

        def ffn_in(mid=None):
            for jx in range(11):
                if jx == 7 and mid is not None:
                    mid()
                w, wB = wget(("FI", jx))
                wv = w[:].rearrange("p (k c) -> p k c", c=512)
                for sub in range(2):
                    hc = 2 * jx + sub
                    b0 = ((2 * jx + sub) % 4) * 2
                    pg, pu = b0, b0 + 1
                    P.op(P.pe, mm_chain(U[pg][:, :], [(wv[:, k, sub * 256: sub * 256 + 128], uT[:, k, :]) for k in range(8)]),
                         reads=[wB] + uTB, writes=[UB[pg]])
                    P.op(P.pe, mm_chain(U[pu][:, :], [(wv[:, k, sub * 256 + 128: sub * 256 + 256], uT[:, k, :]) for k in range(8)]),
                         reads=[wB] + uTB, writes=[UB[pu]])
                    si = hc % 2
                    P.op(P.act, (lambda pg, si: lambda e: e.activation(out=ft[si][:, :], in_=U[pg][:, :], func=AF.Silu))(pg, si),
                         reads=[UB[pg]], writes=[ftB[si]])
                    P.op(P.dve, (lambda pu, si, hc: lambda e: e.tensor_tensor(out=big[:, hc, :], in0=U[pu][:, :], in1=ft[si][:, :], op=ALU.mult))(pu, si, hc),
                         reads=[UB[pu], ftB[si]], writes=[bigB[hc]])

        def load_x(par, seq_i, j, meta):
            if meta:
                P.op(P.sp, lambda e: e.dma_start(out=hb[:, par, 0, :], in_=xmeta), writes=[hB[par][0]], dsem=xS[par])
                P.op(P.sp, lambda e: e.dma_start(out=ropeb[:, par, :, 0:128], in_=roped[:, :, 0:128]), writes=[ropeB[par]], dsem=rS[par])
            else:
                src = xs[seq_i, j * T:(j + 1) * T, :].rearrange("(t p) d -> p t d", p=128)
                P.op(P.sp, lambda e: e.dma_start(out=hb[:, par, :, :], in_=src), writes=hB[par], dsem=xS[par])
                c0 = 128 + j * T
                P.op(P.sp, lambda e: e.dma_start(out=ropeb[:, par, :, :], in_=roped[:, :, c0:c0 + T]), writes=[ropeB[par]], dsem=rS[par])

        def run_interleaved(gens):
            gens = list(gens)
            while gens:
                for g in list(gens):
                    try:
                        next(g)
                    except StopIteration:
                        gens.remove(g)

        work = [("meta", 0, 0)] + [("tile", s, j) for s in range(nseq) for j in range(nsup)]
        if use_order is not None:
            use_seq.extend(use_order)
        load_x(0, 0, 0, True)
        hi = 0
        prepped = {}
        pre_done = {}
        for wi, (kind, s, j) in enumerate(work):
            par = wi % 2
            meta = kind == "meta"
            ntl = 1 if meta else NT
            blk0 = 0 if meta else 1 + j * NT
            if (not meta) and j == 0:
                P.op(P.pool, lambda e: e.tensor_copy(out=S[:], in_=Smeta[:]), reads=[SmetaB], writes=SB)
            uT, uTB = uTm, uTmB
            if not prepped.get(wi, False):
                run_interleaved([chain(norm_stats(par, t), norm_tr(t, 0, uTm, uTmB)) for t in range(ntl)])
            if wi in pre_done:
                cxs = pre_done[wi]
                run_interleaved([hgrn_A(1, ntl, meta, hi, cxs[1])])
                hi += 1
            else:
                cxs = [dict() for _ in range(NH)]
                run_interleaved([hgrn_A(0, ntl, meta, hi, cxs[0])])
                hi += 1
                run_interleaved([hgrn_B(cxs[0]), hgrn_A(1, ntl, meta, hi, cxs[1])])
                hi += 1
            for h in range(NH):
                a_gen = None
                if h + 2 < NH:
                    a_gen = hgrn_A(h + 2, ntl, meta, hi, cxs[h + 2])
                    hi += 1
                gens = [hgrn_C(cxs[h], a_gen)]
                if h + 1 < NH:
                    gens.insert(0, hgrn_B(cxs[h + 1]))
                else:
                    if wi + 1 < len(work):
                        k2, s2, j2 = work[wi + 1]
                        load_x((wi + 1) % 2, s2, j2, k2 == "meta")
                    gens.append(mla_latents(ntl, meta, par, blk0))
                run_interleaved(gens)
            kv_gen(ntl, blk0)
            P.op(P.pool, lambda e: e.memset(dummy[:], 0.0),
                 writes=[b for t in range(ntl) for b in (KnB[blk0 + t], KpeB[blk0 + t], VB[blk0 + t])])
            if meta:
                P.op(P.pool, lambda e: e.tensor_copy(out=Smeta[:], in_=S[:]), reads=SB, writes=[SmetaB])
                continue
            attention(j, par, blk0)
            mix_stage()
            run_interleaved([out_proj(par, [[(("WO", 0), 0, 8)], [(("WO", 1), 0, 8)]], 16, 0, False, s, j * T,
                                      after_tile=lambda t: chain(norm_stats(par, t), norm_tr(t, 8, uTf, uTfB)))])
            uT, uTB = uTf, uTfB
            nxt = wi + 1 < len(work) and work[wi + 1][0] == "tile"
            ffn_in(mid=(lambda: run_interleaved([norm_stats((wi + 1) % 2, t) for t in range(NT)])) if nxt else None)
            if nxt:
                run_interleaved([norm_tr(t, 0, uTm, uTmB) for t in range(NT)])
                prepped[wi + 1] = True
            fo = out_proj(par, [[(("FO", hf * 3 + g), k0, k1) for g, (k0, k1) in enumerate(FO_GROUPS)] for hf in range(2)],
                          0, D, True, s, j * T)
            if nxt:
                uT, uTB = uTm, uTmB
                ncx = [dict() for _ in range(NH)]

                def pre(hi0=hi, ncx=ncx):
                    yield from hgrn_A(0, NT, False, hi0, ncx[0])
                    yield from hgrn_B(ncx[0])
                hi += 1
                run_interleaved([fo, pre()])
                pre_done[wi + 1] = ncx
            else:
                run_interleaved([fo])
        if use_order is None:
            return recorded
        assert state["use"] == len(use_seq) and state["load"] == len(use_seq) and not held
        P.wait_only(P.pool, [(d.sem, d.val) for row in stS for d in row if d.val > 0])

        with nc.Block() as block:
            @block.tensor
            def _(e):
                _emit(P.pe, e)

            @block.scalar
            def _(e):
                _emit(P.act, e)

            @block.vector
            def _(e):
                _emit(P.dve, e)

            @block.gpsimd
            def _(e):
                _emit(P.pool, e)

            @block.sync
            def _(e):
                _emit(P.sp, e)
    return nc


def _host_inputs(x, meta_tokens, w_in, b_gate, lb_logits, hg_norm_g, w_hg_o, q_a_norm_g, w_q_b, kv_a_norm_g,
                 w_kv_b, w_mla_o, w_out, mix_pre_g, mix_post_g, ffn_pre_g, ffn_post_g, w_ffn_in, w_ffn_out):
    f = lambda a: np.ascontiguousarray(np.asarray(a, dtype=np.float32))
    wp = _pack_weights(f(w_in)[0], f(w_hg_o)[0], f(w_q_b)[0], f(w_kv_b)[0], f(w_mla_o)[0], f(w_out)[0],
                       f(w_ffn_in)[0], f(w_ffn_out)[0])
    cfm = np.zeros((128, NCF), np.float32)
    fm = lambda v: f(v).reshape(-1, 128).T
    cfm[:, 0:8] = fm(mix_pre_g[0])
    cfm[:, 8:16] = fm(ffn_pre_g[0])
    cfm[:, 16:24] = fm(lb_logits[0])
    cfm[:, 24:32] = fm(lb_logits[1])
    cfm[:, 32:33] = fm(hg_norm_g[0])
    cfm[:, 33:35] = fm(q_a_norm_g[0])
    cfm[:, 35:37] = fm(kv_a_norm_g[0])
    cfm[:, 37:53] = fm(b_gate[0])
    cfm[112:, 53] = 1.0
    gp = np.concatenate([f(mix_post_g[0]), f(ffn_post_g[0])])[None, :].repeat(128, axis=0)
    xm = np.zeros((128, D), np.float32)
    xm[112:] = f(meta_tokens)
    common = {"xmeta": xm, "wpack": wp, "cf32": cfm, "gpost": np.ascontiguousarray(gp),
              "rope": _rope_tables(), "cmat": _const_mats()}
    xf = f(x)
    return [dict(common, xs=np.ascontiguousarray(xf[c * NSEQ:(c + 1) * NSEQ])) for c in range(NCORES)]


def kernel(**inputs):
    in_maps = _host_inputs(**inputs)
    nc = build_program()
    res = run_bass_kernel_spmd(nc, in_maps, core_ids=list(range(NCORES)))
    return np.concatenate([np.asarray(r["out"], dtype=np.float32) for r in res.results], axis=0)
```

```python
import math
from contextlib import ExitStack
import numpy as np
import concourse.bass as bass
import concourse.mybir as mybir
from concourse.bass_utils import run_bass_kernel_spmd

F32 = mybir.dt.float32
BF16 = mybir.dt.bfloat16
AF = mybir.ActivationFunctionType
ALU = mybir.AluOpType

NCORES = 8
BATCH = 32
SEQ = 2048
D = 1024
NSEQ = BATCH // NCORES
NT = 2
T = NT * 128
NSUP = SEQ // T
NH = 8
EPS = 1e-6
FFN_H = 2816
NKC_F = FFN_H // 128
NBLK = 1 + SEQ // 128
SCALE = (128 + 64) ** -0.5
NSLOT = 4
CH = 4096

CH_H = [("H", h) for h in range(NH)]
CH_L = [("L1", 0), ("L2", 0)]
CH_KV = [("WKV", 0)]
CH_Q = [("WQ", 0)]
CH_MX = [("MX", f) for f in range(8)]
CH_WO = [("WO", hf) for hf in range(2)]
CH_FI = [("FI", j) for j in range(11)]
FO_GROUPS = [(0, 8), (8, 16), (16, 22)]
CH_FO = [("FO", hf * 3 + g) for hf in range(2) for g in range(3)]
CHUNKS = CH_H + CH_L + CH_KV + CH_Q + CH_MX + CH_WO + CH_FI + CH_FO
CHIDX = {c: i for i, c in enumerate(CHUNKS)}
NCH = len(CHUNKS)
SEQ_FULL = [CHIDX[c] for c in CHUNKS]
SEQ_META = [CHIDX[c] for c in CH_H + CH_L + CH_KV]

NCF = 64


def _pack_weights(w_in, w_hg_o, w_q_b, w_kv_b, w_mla_o, w_out, w_ffn_in, w_ffn_out):
    wp = np.zeros((NCH, 128, CH), np.float32)

    def put(ci, mat):
        K, nco = mat.shape
        kc = K // 128
        wp[ci, :, : kc * nco] = mat.reshape(kc, 128, nco).transpose(1, 0, 2).reshape(128, kc * nco)

    sw = np.concatenate([np.arange(32, 64), np.arange(0, 32)])
    for h in range(NH):
        cols = np.concatenate([np.arange(h * 128, (h + 1) * 128) + off for off in (0, 1024, 3072, 2048)])
        put(CHIDX[("H", h)], w_in[:, cols])
    put(CHIDX[("L1", 0)], w_in[:, 4096:4608])
    kpe_cols = np.arange(4608, 4672)
    put(CHIDX[("L2", 0)], w_in[:, np.concatenate([kpe_cols, kpe_cols[sw]])])
    cols = []
    for h in range(NH):
        base = h * 192
        cols += [np.arange(base, base + 128), np.arange(base + 128, base + 192), (np.arange(base + 128, base + 192))[sw]]
    put(CHIDX[("WQ", 0)], w_q_b[:, np.concatenate(cols)])
    kcols = np.concatenate([np.arange(h * 256, h * 256 + 128) for h in range(NH)])
    vcols = np.concatenate([np.arange(h * 256 + 128, h * 256 + 256) for h in range(NH)])
    put(CHIDX[("WKV", 0)], w_kv_b[:, np.concatenate([kcols, vcols])])
    for f in range(8):
        sl = slice(f * 128, (f + 1) * 128)
        m = np.concatenate([w_hg_o[:, sl], w_mla_o[:, sl], w_in[:, 4672 + f * 128: 4672 + (f + 1) * 128],
                            w_in[:, 5696 + f * 128: 5696 + (f + 1) * 128]], axis=1)
        put(CHIDX[("MX", f)], m)
    for hf in range(2):
        put(CHIDX[("WO", hf)], w_out[:, hf * 512:(hf + 1) * 512])
    for j in range(11):
        a, b = 2 * j, 2 * j + 1
        m = np.concatenate([w_ffn_in[:, a * 128:(a + 1) * 128], w_ffn_in[:, FFN_H + a * 128: FFN_H + (a + 1) * 128],
                            w_ffn_in[:, b * 128:(b + 1) * 128], w_ffn_in[:, FFN_H + b * 128: FFN_H + (b + 1) * 128]], axis=1)
        put(CHIDX[("FI", j)], m)
    for hf in range(2):
        for g, (k0, k1) in enumerate(FO_GROUPS):
            put(CHIDX[("FO", hf * 3 + g)], w_ffn_out[k0 * 128:k1 * 128, hf * 512:(hf + 1) * 512])
    return wp


def _rope_tables():
    pos = np.concatenate([np.zeros(112, np.float32), np.arange(16, dtype=np.float32),
                          np.arange(16, 16 + SEQ, dtype=np.float32)])
    inv = (1.0 / (np.float32(10000.0) ** (np.arange(0, 64, 2, dtype=np.float32) / np.float32(64)))).astype(np.float32)
    ang = (pos[None, :] * inv[:, None]).astype(np.float32)
    c, s = np.cos(ang).astype(np.float32), np.sin(ang).astype(np.float32)
    tab = np.zeros((64, 2, pos.shape[0]), np.float32)
    tab[:32, 0], tab[32:, 0] = c, c
    tab[:32, 1], tab[32:, 1] = -s, s
    return tab


def _const_mats():
    m = np.zeros((128, 384), np.float32)
    m[:, 0:128] = np.eye(128, dtype=np.float32)
    m[:, 128:256] = np.triu(np.ones((128, 128), np.float32))
    m[112:, 256:384] = 1.0
    return m


class Eng:
    def __init__(self, name, is_pe=False):
        self.name, self.is_pe = name, is_pe
        self.ops, self.sem, self.cnt, self.seen = [], None, 0, {}


class Buf:
    __slots__ = ("name", "w", "r", "excl")

    def __init__(self, name, excl=False):
        self.name, self.w, self.r, self.excl = name, None, {}, excl


class DSem:
    def __init__(self, sem):
        self.sem, self.val = sem, 0


class Prog:
    def __init__(self):
        self.pe, self.act, self.dve = Eng("pe", True), Eng("act"), Eng("dve")
        self.pool, self.sp = Eng("pool"), Eng("sp")

    def op(self, eng, fn, reads=(), writes=(), dsem=None):
        waits = {}

        def need(tok):
            if tok is None:
                return
            sem, val = tok
            if eng.is_pe and sem is eng.sem:
                return
            if eng.seen.get(sem, 0) >= val:
                return
            if waits.get(sem, 0) < val:
                waits[sem] = val

        writes = list({id(b): b for b in list(writes) + [b for b in reads if b.excl]}.values())
        reads = list({id(b): b for b in reads if not b.excl}.values())
        for b in reads:
            need(b.w)
        for b in writes:
            need(b.w)
            for t in b.r.items():
                need(t)
        for sem, val in waits.items():
            eng.seen[sem] = val
        if dsem is not None:
            dsem.val += 16
            tok, inc = (dsem.sem, dsem.val), 16
        else:
            eng.cnt += 1
            tok, inc = (eng.sem, eng.cnt), 1
        for b in reads:
            if b.r.get(tok[0], 0) < tok[1]:
                b.r[tok[0]] = tok[1]
        for b in writes:
            b.w, b.r = tok, {}
        eng.ops.append((list(waits.items()), fn, tok[0], inc))
        return tok

    def wait_only(self, eng, toks):
        waits = {}
        for sem, val in toks:
            if eng.seen.get(sem, 0) < val and waits.get(sem, 0) < val:
                waits[sem] = val
        for sem, val in waits.items():
            eng.seen[sem] = val
        eng.ops.append((list(waits.items()), None, None, 0))


def _emit(eng, e):
    for waits, fn, sem, inc in eng.ops:
        for s, v in waits:
            e.wait_ge(s, v)
        if fn is not None:
            fn(e).then_inc(sem, inc)


def build_program(nseq=NSEQ, nsup=NSUP):
    order = _build(nseq, nsup, None)
    return _build(nseq, nsup, order)


def _build(nseq, nsup, use_order):
    nc = bass.Bass("TRN2", target_bir_lowering=False)
    xs = nc.dram_tensor("xs", [NSEQ, SEQ, D], F32, kind="ExternalInput").ap()
    xmeta = nc.dram_tensor("xmeta", [128, D], F32, kind="ExternalInput").ap()
    wpack = nc.dram_tensor("wpack", [NCH, 128, CH], F32, kind="ExternalInput").ap()
    cf32d = nc.dram_tensor("cf32", [128, NCF], F32, kind="ExternalInput").ap()
    gpostd = nc.dram_tensor("gpost", [128, 2 * D], F32, kind="ExternalInput").ap()
    roped = nc.dram_tensor("rope", [64, 2, 128 + SEQ], F32, kind="ExternalInput").ap()
    cmatd = nc.dram_tensor("cmat", [128, 384], F32, kind="ExternalInput").ap()
    outd = nc.dram_tensor("out", [NSEQ, SEQ, D], F32, kind="ExternalOutput").ap()
    wbf = nc.dram_tensor("wbf", [NCH, 128, CH], BF16, kind="Internal").ap()

    P = Prog()
    with ExitStack() as es:
        def sb(name, shape, dt):
            return es.enter_context(nc.sbuf_tensor(name, shape, dt))

        def sem(name):
            return es.enter_context(nc.semaphore(name))

        for e in (P.pe, P.act, P.dve, P.pool, P.sp):
            e.sem = sem("s_" + e.name)

        ring = [sb(f"ring{i}", [128, CH], BF16) for i in range(NSLOT)]
        ringB = [Buf(f"ring{i}") for i in range(NSLOT)]
        ringS = [DSem(sem(f"ringsem{i}")) for i in range(NSLOT)]
        Kn = sb("Kn", [128, NH, NBLK * 128], BF16)
        Kpe = sb("Kpe", [128, NBLK * 128], BF16)
        Vc = sb("Vc", [128, NBLK, D], BF16)
        KnB = [Buf(f"Kn{b}") for b in range(NBLK)]
        KpeB = [Buf(f"Kpe{b}") for b in range(NBLK)]
        VB = [Buf(f"V{b}") for b in range(NBLK)]
        hb = sb("h", [128, 2, NT, D], F32)
        hB = [[Buf(f"h{p}_{t}") for t in range(NT)] for p in range(2)]
        xS = [DSem(sem(f"xsem{p}")) for p in range(2)]
        stS = [[DSem(sem(f"stsem{p}_{t}")) for t in range(NT)] for p in range(2)]
        ropeb = sb("ropeb", [64, 2, 2, T], F32)
        ropeB = [Buf(f"rope{p}") for p in range(2)]
        rS = [DSem(sem(f"rsem{p}")) for p in range(2)]
        uTm = sb("uTm", [128, 8, T], BF16)
        uTmB = [Buf(f"uTm{t}") for t in range(NT)]
        uTf = sb("uTf", [128, 8, T], BF16)
        uTfB = [Buf(f"uTf{t}") for t in range(NT)]
        uT, uTB = uTm, uTmB
        xn2 = sb("xn2", [128, NT, D], BF16)
        xn2B = [Buf(f"xn2_{t}") for t in range(NT)]
        small = sb("small", [128, 64], F32)
        smallB = {}

        def sm(name):
            if name not in smallB:
                smallB[name] = (len(smallB), Buf("sm_" + name))
            i, b = smallB[name]
            return small[:, i:i + 1], b

        NFT = 12
        ft = [sb(f"ft{i}", [128, T], F32) for i in range(NFT)]
        ftB = [Buf(f"ft{i}") for i in range(NFT)]
        NBT = 9
        bt = [sb(f"bt{i}", [128, T], BF16) for i in range(NBT)]
        btB = [Buf(f"bt{i}") for i in range(NBT)]
        tmpM = sb("tmpM", [128, 128], F32)
        tmpMB = Buf("tmpM")
        Sbf = sb("Sbf", [128, 128], BF16)
        SbfB = Buf("Sbf")
        S = sb("S", [128, NH, 128], F32)
        SB = [Buf(f"S{h}") for h in range(NH)]
        Smeta = sb("Smeta", [128, NH, 128], F32)
        SmetaB = Buf("Smeta")
        big = sb("big", [128, 24, T], BF16)
        bigB = [Buf(f"big{i}") for i in range(24)]
        qn = sb("qn", [128, NH, T], BF16)
        qnB = [Buf(f"qn{h}") for h in range(NH)]
        qpe = sb("qpe", [128, NH, T], BF16)
        qpeB = [Buf(f"qpe{h}") for h in range(NH)]
        cqn = sb("cqn", [128, 2, T], BF16)
        cqnB = Buf("cqn")
        ckvn = sb("ckvn", [128, 2, T], BF16)
        ckvnB = Buf("ckvn")
        PT = [sb(f"PT{i}", [128, 2 * T], BF16) for i in range(4)]
        PTB = [Buf(f"PT{i}") for i in range(4)]
        y0 = sb("y0", [128, NT, 512], F32)
        y0B = [Buf(f"y0_{t}") for t in range(NT)]
        gpostb = sb("gpostb", [128, 2 * D], F32)
        cf = sb("cf_sb", [128, NCF], F32)
        cd = sb("cd", [128, 32], F32)
        cmat = sb("cmat_sb", [128, 384], BF16)
        onesf = sb("onesf", [128, 128], F32)
        dummy = sb("dummy_sb", [128, 8], F32)

        constB = Buf("const")
        cS = DSem(sem("csem"))
        ident, triu, ones_meta = cmat[:, 0:128], cmat[:, 128:256], cmat[:, 256:384]

        psF = [es.enter_context(nc.psum_tensor(f"psF{i}", [128, 512], F32)) for i in range(7)]
        psF.append(es.enter_context(nc.psum_tensor("psF7", [128, 512], F32)))
        psBt = psF[7][:, :].bitcast(BF16)
        U = [psF[i // 2][:, (i % 2) * 256:(i % 2) * 256 + 256] for i in range(14)]
        bankB = [Buf(f"bank{i}", excl=True) for i in range(8)]
        UB = [bankB[i // 2] for i in range(14)]
        PB = [psBt[:, q * 256:(q + 1) * 256] for q in range(4)]
        PBB = [bankB[7] for q in range(4)]
        pb_rr = [0]

        def next_pb():
            q = pb_rr[0] % 4
            pb_rr[0] += 1
            return PB[q], PBB[q]

        P.op(P.sp, lambda e: e.dma_start(out=cf[:], in_=cf32d), writes=[constB], dsem=cS)
        P.op(P.sp, lambda e: e.dma_start(out=gpostb[:], in_=gpostd), writes=[constB], dsem=cS)
        cS2 = DSem(sem("csem2"))
        cmB = Buf("cmatB")
        P.op(P.pool, lambda e: e.dma_start(out=cmat[:], in_=cmatd), writes=[cmB], dsem=cS2)
        P.op(P.pool, lambda e: e.memset(onesf[:], 1.0), reads=[cmB], writes=[constB])
        P.op(P.pool, lambda e: e.memset(S[:], 0.0), writes=SB)
        P.op(P.pool, lambda e: e.memset(Kpe[64:128, :], 0.0), writes=KpeB)
        P.op(P.pool, lambda e: e.memset(qpe[64:128, :, :], 0.0), writes=qpeB)
        P.op(P.dve, lambda e: e.tensor_tensor(out=cd[:, 0:8], in0=cf[:, 16:24], in1=cf[:, 24:32], op=ALU.subtract),
             reads=[constB], writes=[constB])
        P.op(P.act, lambda e: e.activation(out=cd[:, 0:8], in_=cd[:, 0:8], func=AF.Sigmoid), reads=[constB], writes=[constB])
        P.op(P.dve, lambda e: e.tensor_scalar(out=cd[:, 8:16], in0=cd[:, 0:8], scalar1=-1.0, scalar2=1.0,
                                              op0=ALU.mult, op1=ALU.add), reads=[constB], writes=[constB])
        lb_ap = lambda h: cd[:, h:h + 1]
        oml_ap = lambda h: cd[:, 8 + h:9 + h]

        NWG = 13
        wS = [DSem(sem(f"wsem{i}")) for i in range(NWG)]
        wbfB = [Buf(f"wbf{c}") for c in range(NCH)]
        for c in range(NCH):
            g = c // 3
            P.op(P.pool, (lambda c: lambda e: e.dma_start(out=wbf[c], in_=wpack[c]))(c), writes=[], dsem=wS[g])
        for c in range(NCH):
            g = c // 3
            wbfB[c].w = (wS[g].sem, wS[g].val)

        use_seq = []
        state = {"use": 0, "load": 0}

        def issue_load(i):
            c = use_seq[i]
            s = i % NSLOT
            P.op(P.sp, (lambda c, s: lambda e: e.dma_start(out=ring[s][:], in_=wbf[c]))(c, s),
                 reads=[wbfB[c]], writes=[ringB[s]], dsem=ringS[s])

        recorded = []
        held = set()

        def try_loads(k):
            while state["load"] < min(len(use_seq), k + NSLOT) and (state["load"] - NSLOT) not in held:
                issue_load(state["load"])
                state["load"] += 1

        def wget(expect, hold=False):
            k = state["use"]
            state["use"] += 1
            state["lastk"] = k
            recorded.append(CHIDX[expect])
            if hold:
                held.add(k)
            if use_order is not None:
                assert use_seq[k] == CHIDX[expect], (k, use_seq[k], expect)
                try_loads(k)
            s = k % NSLOT
            return ring[s], ringB[s]

        def release(k):
            held.discard(k)
            if use_order is not None:
                try_loads(state["use"] - 1)

        def mm_chain(out_ap, pairs):
            def fn(e):
                ins = None
                n = len(pairs)
                for i, (l, r) in enumerate(pairs):
                    ins = e.matmul(out_ap, lhsT=l, rhs=r, start=(i == 0), stop=(i == n - 1))
                return ins
            return fn

        def rstd_from(ss_ap, ss_bufs, n, out_ap, out_buf, width_bufs=()):
            P.op(P.act, lambda e: e.activation(out=out_ap, in_=ss_ap, func=AF.Ln, scale=1.0 / n, bias=EPS),
                 reads=ss_bufs, writes=[out_buf])
            P.op(P.act, lambda e: e.activation(out=out_ap, in_=out_ap, func=AF.Exp, scale=-0.5),
                 reads=[out_buf], writes=[out_buf])

        def norm_stats(par, t):
            hx = hb[:, par, t, :]
            ss_ap, ssB = sm(f"nt_ss{t}")
            rs_ap, rsB = sm(f"nt_rs{t}")
            P.op(P.act, lambda e: e.activation(out=xn2[:, t, :], in_=hx, func=AF.Square, accum_out=ss_ap),
                 reads=[hB[par][t]], writes=[xn2B[t], ssB])
            yield
            rstd_from(ss_ap, [ssB], D, rs_ap, rsB)
            yield
            P.op(P.act, lambda e: e.activation(out=xn2[:, t, :], in_=hx, func=AF.Copy, scale=rs_ap),
                 reads=[hB[par][t], rsB], writes=[xn2B[t]])
            yield

        def norm_tr(t, gcol0, uTx, uTxB):
            pb, pb2 = PB[2 * (t % 2)], PB[2 * (t % 2) + 1]
            pbB = pbB2 = PBB[0]
            for g4 in range(2):
                def fn(e, g4=g4):
                    ins = None
                    for i in range(4):
                        kc = g4 * 4 + i
                        dst = (pb if i < 2 else pb2)[:, (i % 2) * 128:(i % 2) * 128 + 128]
                        ins = e.transpose(dst, xn2[:, t, kc * 128:(kc + 1) * 128], ident)
                    return ins
                P.op(P.pe, fn, reads=[xn2B[t], constB], writes=[pbB])
                yield
                for half, pp in enumerate((pb, pb2)):
                    kc0 = g4 * 4 + half * 2
                    P.op(P.dve, (lambda pp, kc0: lambda e: e.tensor_tensor(
                        out=uTx[:, kc0:kc0 + 2, t * 128:(t + 1) * 128],
                        in0=pp.rearrange("p (c t) -> p c t", t=128),
                        in1=cf[:, gcol0 + kc0:gcol0 + kc0 + 2].unsqueeze(2).to_broadcast([128, 2, 128]),
                        op=ALU.mult))(pp, kc0), reads=[pbB, constB], writes=[uTxB[t]])
                    yield

        def chain(*gens):
            for g in gens:
                yield from g

        def rr_merge(gens):
            gens = list(gens)
            while gens:
                for g in list(gens):
                    try:
                        next(g)
                        yield
                    except StopIteration:
                        gens.remove(g)

        def hgrn_A(h, ntl, meta, hi, cx):
            Tn = ntl * 128
            pp = hi % 2
            _, emidB = sm(f"emid{pp}_0")
            for c in range(1, NT):
                sm(f"emid{pp}_{c}")
            _, elB = sm(f"elast{pp}_0")
            for c in range(1, NT):
                sm(f"elast{pp}_{c}")
            ei = smallB[f"emid{pp}_0"][0]
            li = smallB[f"elast{pp}_0"][0]
            cx.update(h=h, ntl=ntl, meta=meta, pp=pp, ei=ei, li=li, emidB=emidB, elB=elB)
            w, wB = wget(("H", h), hold=True)
            kk = state["lastk"]
            wv = w[:].rearrange("p (k c) -> p k c", c=512)
            base = 0 if pp == 0 else 8
            uq, uf, ug, uv = (base + i for i in range(4))
            uTr = uTB[:ntl]
            if not meta:
                P.op(P.pe, mm_chain(U[uq][:, :Tn], [(wv[:, k, 0:128], uT[:, k, :Tn]) for k in range(8)]),
                     reads=[wB] + uTr, writes=[UB[uq]])
                yield
            P.op(P.pe, mm_chain(U[uf][:, :Tn], [(wv[:, k, 128:256], uT[:, k, :Tn]) for k in range(8)]),
                 reads=[wB] + uTr, writes=[UB[uf]])
            yield
            if not meta:
                P.op(P.pe, mm_chain(U[ug][:, :Tn], [(wv[:, k, 256:384], uT[:, k, :Tn]) for k in range(8)]),
                     reads=[wB] + uTr, writes=[UB[ug]])
                yield
            for t in range(ntl):
                P.op(P.pe, mm_chain(U[uv][:, t * 128:(t + 1) * 128],
                                    [(uT[:, k, t * 128:(t + 1) * 128], wv[:, k, 384:512]) for k in range(8)]),
                     reads=[wB, uTB[t]], writes=[UB[uv]])
                if t == ntl - 1:
                    release(kk)
                yield

        def hgrn_B(cx):
            h, ntl, meta, pp, ei, li, emidB, elB = (cx[k] for k in ("h", "ntl", "meta", "pp", "ei", "li", "emidB", "elB"))
            Tn = ntl * 128
            base = 0 if pp == 0 else 8
            uq, uf, ug, uv = (base + i for i in range(4))
            QH, FG, LF, KK, BM, EI = 0, 1, 2, 3, 4, 7
            E_, GS = 5 + pp, 8 + pp
            QD, KD, VBt = 0 + pp, 2 + pp, 4 + pp
            f = lambda i: ft[i][:, :Tn]
            b_ = lambda i: bt[i][:, :Tn]
            def sig3(dst, dstB, src, srcB):
                P.op(P.act, lambda e: e.activation(out=f(dst), in_=src, func=AF.Exp, scale=-1.0), reads=[srcB], writes=[ftB[dst]])
                yield
                P.op(P.act, lambda e: e.activation(out=f(dst), in_=f(dst), func=AF.Ln, scale=1.0, bias=1.0), reads=[ftB[dst]], writes=[ftB[dst]])
                yield
                P.op(P.act, lambda e: e.activation(out=f(dst), in_=f(dst), func=AF.Exp, scale=-1.0), reads=[ftB[dst]], writes=[ftB[dst]])
                yield
            yield from sig3(FG, None, U[uf][:, :Tn], UB[uf])
            if not meta:
                yield from sig3(QH, None, U[uq][:, :Tn], UB[uq])
                yield from sig3(GS, None, U[ug][:, :Tn], UB[ug])
                P.op(P.dve, lambda e: e.tensor_tensor(out=f(QH), in0=U[uq][:, :Tn], in1=f(QH), op=ALU.mult),
                     reads=[UB[uq], ftB[QH]], writes=[ftB[QH]])
                yield
            P.op(P.dve, lambda e: e.tensor_scalar(out=f(FG), in0=f(FG), scalar1=oml_ap(h), scalar2=lb_ap(h),
                                                  op0=ALU.mult, op1=ALU.add), reads=[ftB[FG], constB], writes=[ftB[FG]])
            yield
            P.op(P.act, lambda e: e.activation(out=f(LF), in_=f(FG), func=AF.Ln), reads=[ftB[FG]], writes=[ftB[LF]])
            yield
            P.op(P.pool, lambda e: e.tensor_scalar(out=f(KK), in0=f(FG), scalar1=-1.0, scalar2=1.0, op0=ALU.mult, op1=ALU.add),
                 reads=[ftB[FG]], writes=[ftB[KK]])
            yield

            def scan_fn(e):
                ins = None
                for c in range(ntl):
                    cs = slice(c * 128, (c + 1) * 128)
                    ins = e.tensor_tensor_scan(out=ft[LF][:, cs], data0=onesf[:], data1=ft[LF][:, cs], initial=0.0,
                                               op0=ALU.mult, op1=ALU.add)
                return ins
            P.op(P.dve, scan_fn, reads=[ftB[LF], constB], writes=[ftB[LF]])
            yield
            lfv = ft[LF][:, :Tn].rearrange("p (c t) -> p c t", t=128)
            P.op(P.dve, lambda e: e.tensor_tensor(out=ft[BM][:, :Tn].rearrange("p (c t) -> p c t", t=128), in0=lfv,
                                                  in1=lfv[:, :, 63:64].to_broadcast([128, ntl, 128]), op=ALU.subtract),
                 reads=[ftB[LF]], writes=[ftB[BM]])
            yield
            P.op(P.act, lambda e: e.activation(out=f(E_), in_=f(BM), func=AF.Exp), reads=[ftB[BM]], writes=[ftB[E_]])
            yield
            P.op(P.act, lambda e: e.activation(out=f(EI), in_=f(BM), func=AF.Exp, scale=-1.0), reads=[ftB[BM]], writes=[ftB[EI]])
            yield
            emid = small[:, ei:ei + ntl]
            elast = small[:, li:li + ntl]
            P.op(P.act, lambda e: e.activation(out=emid.unsqueeze(2), in_=lfv[:, :, 63:64], func=AF.Exp),
                 reads=[ftB[LF]], writes=[emidB])
            yield
            if not meta:
                P.op(P.dve, lambda e: e.tensor_tensor(out=f(GS), in0=U[ug][:, :Tn], in1=f(GS), op=ALU.mult),
                     reads=[UB[ug], ftB[GS]], writes=[ftB[GS]])
                yield
            P.op(P.dve, lambda e: e.tensor_copy(out=b_(VBt), in_=U[uv][:, :Tn]), reads=[UB[uv]], writes=[btB[VBt]])
            yield
            Ev = ft[E_][:, :Tn].rearrange("p (c t) -> p c t", t=128)
            P.op(P.dve, lambda e: e.tensor_tensor(out=elast.unsqueeze(2), in0=emid.unsqueeze(2), in1=Ev[:, :, 127:128], op=ALU.mult),
                 reads=[emidB, ftB[E_]], writes=[elB])
            yield
            if not meta:
                P.op(P.dve, lambda e: e.tensor_tensor(out=b_(QD), in0=f(QH), in1=f(E_), op=ALU.mult),
                     reads=[ftB[QH], ftB[E_]], writes=[btB[QD]])
                yield
            P.op(P.dve, lambda e: e.tensor_tensor(out=b_(KD), in0=f(KK), in1=f(EI), op=ALU.mult),
                 reads=[ftB[KK], ftB[EI]], writes=[btB[KD]])
            yield

        def hgrn_C(cx, a_gen=None):
            def ains():
                if a_gen is not None:
                    next(a_gen, None)
            h, ntl, meta, pp, ei, li, emidB, elB = (cx[k] for k in ("h", "ntl", "meta", "pp", "ei", "li", "emidB", "elB"))
            Tn = ntl * 128
            E_, GS = 5 + pp, 8 + pp
            QD, KD, VBt = 0 + pp, 2 + pp, 4 + pp
            KDT, AM, SQ = 6, 7, 8
            T1, RS = 10, 11
            f = lambda i: ft[i][:, :Tn]
            b_ = lambda i: bt[i][:, :Tn]
            pb, pbB = next_pb()

            def tr_fn(e):
                ins = None
                for c in range(ntl):
                    ins = e.transpose(pb[:, c * 128:(c + 1) * 128], bt[KD][:, c * 128:(c + 1) * 128], ident)
                return ins
            P.op(P.pe, tr_fn, reads=[btB[KD], constB], writes=[pbB])
            yield
            P.op(P.dve, lambda e: e.tensor_copy(out=b_(KDT), in_=pb[:, :Tn]), reads=[pbB], writes=[btB[KDT]])
            yield
            UA, UO, USS, UM = 4, 5, 6, 7
            if meta:
                ains()
            if not meta:
                def at_fn(e):
                    ins = None
                    for c in range(ntl):
                        cs = slice(c * 128, (c + 1) * 128)
                        ins = e.matmul(U[UA][:, cs], lhsT=bt[KD][:, cs], rhs=bt[QD][:, cs], start=True, stop=True)
                    return ins
                P.op(P.pe, at_fn, reads=[btB[KD], btB[QD]], writes=[UB[UA]])
                ains()
                yield
                P.op(P.dve, lambda e: e.tensor_tensor(out=bt[AM][:, :Tn].rearrange("p (c t) -> p c t", t=128),
                                                      in0=U[UA][:, :Tn].rearrange("p (c t) -> p c t", t=128),
                                                      in1=triu.unsqueeze(1).to_broadcast([128, ntl, 128]), op=ALU.mult),
                     reads=[UB[UA], constB], writes=[btB[AM]])
                yield
            for c in range(ntl):
                cs = slice(c * 128, (c + 1) * 128)
                if not meta:
                    P.op(P.dve, (lambda c: lambda e: e.tensor_scalar(out=Sbf[:], in0=S[:, h, :], scalar1=small[:, ei + c:ei + c + 1],
                                                                       scalar2=None, op0=ALU.mult))(c),
                         reads=[SB[h], emidB], writes=[SbfB])
                    yield
                    P.op(P.pe, (lambda cs: lambda e: (e.matmul(U[UO][:, cs], lhsT=bt[VBt][:, cs], rhs=bt[AM][:, cs], start=True, stop=False),
                                                       e.matmul(U[UO][:, cs], lhsT=Sbf[:], rhs=bt[QD][:, cs], start=False, stop=True))[1])(cs),
                         reads=[btB[VBt], btB[AM], SbfB, btB[QD]], writes=[UB[UO]])
                    yield
                P.op(P.pe, (lambda c, cs: lambda e: e.matmul(U[UM][:, 0:128], lhsT=bt[KDT][:, cs], rhs=bt[VBt][:, cs], start=True, stop=True))(c, cs),
                     reads=[btB[KDT], btB[VBt]], writes=[UB[UM]])
                ains()
                yield
                P.op(P.act, (lambda c: lambda e: e.activation(out=tmpM[:], in_=U[UM][:, 0:128], func=AF.Copy,
                                                              scale=ft[E_][:, c * 128 + 127:c * 128 + 128]))(c),
                     reads=[UB[UM], ftB[E_]], writes=[tmpMB])
                yield
                P.op(P.dve, (lambda c: lambda e: e.scalar_tensor_tensor(out=S[:, h, :], in0=S[:, h, :], scalar=small[:, li + c:li + c + 1],
                                                                         in1=tmpM[:], op0=ALU.mult, op1=ALU.add))(c),
                     reads=[SB[h], elB, tmpMB], writes=[SB[h]])
                yield
            if meta:
                while a_gen is not None and next(a_gen, "done") != "done":
                    pass
                return
            P.op(P.act, lambda e: e.activation(out=b_(SQ), in_=U[UO][:, :Tn], func=AF.Square), reads=[UB[UO]], writes=[btB[SQ]])
            ains()
            yield
            P.op(P.pe, lambda e: e.matmul(U[USS][:, :Tn], lhsT=cmat_ones(), rhs=b_(SQ), start=True, stop=True),
                 reads=[btB[SQ], constB], writes=[UB[USS]])
            while a_gen is not None and next(a_gen, "done") != "done":
                pass
            yield
            rstd_from(U[USS][:, :Tn], [UB[USS]], 128, f(RS), ftB[RS])
            yield
            P.op(P.dve, lambda e: e.scalar_tensor_tensor(out=f(T1), in0=U[UO][:, :Tn], scalar=cf[:, 32:33], in1=f(RS),
                                                          op0=ALU.mult, op1=ALU.mult),
                 reads=[UB[UO], ftB[RS], constB], writes=[ftB[T1]])
            yield
            P.op(P.dve, lambda e: e.tensor_tensor(out=big[:, h, :Tn], in0=f(T1), in1=f(GS), op=ALU.mult),
                 reads=[ftB[T1], ftB[GS]], writes=[bigB[h]])
            yield

        onesb = sb("onesb", [128, 128], BF16)
        P.op(P.pool, lambda e: e.memset(onesb[:], 1.0), writes=[constB])
        cmat_ones = lambda: onesb[:]

        def rope_apply(ps_a, ps_b, bufs_in, rpar, Tn, out_ap, out_bufs, fa=8):
            F_A, F_B = fa, fa + 1
            P.op(P.dve, lambda e: e.tensor_tensor(out=ft[F_A][0:64, :Tn], in0=ps_a, in1=ropeb[:, rpar, 0, :Tn], op=ALU.mult),
                 reads=bufs_in[0:1] + [ropeB[rpar]], writes=[ftB[F_A]])
            P.op(P.dve, lambda e: e.tensor_tensor(out=ft[F_B][0:64, :Tn], in0=ps_b, in1=ropeb[:, rpar, 1, :Tn], op=ALU.mult),
                 reads=bufs_in[1:2] + [ropeB[rpar]], writes=[ftB[F_B]])
            P.op(P.pool, lambda e: e.tensor_tensor(out=out_ap, in0=ft[F_A][0:64, :Tn], in1=ft[F_B][0:64, :Tn], op=ALU.add),
                 reads=[ftB[F_A], ftB[F_B]], writes=out_bufs)

        def latent_norm(ps0, ps1, psB_, ussi, gcol, out_t, out_B, Tn):
            SQ0, SQ1 = 0, 2
            RS = 7
            P.op(P.act, lambda e: e.activation(out=bt[SQ0][:, :Tn], in_=ps0, func=AF.Square), reads=[psB_[0]], writes=[btB[SQ0]])
            yield
            P.op(P.act, lambda e: e.activation(out=bt[SQ1][:, :Tn], in_=ps1, func=AF.Square), reads=[psB_[1]], writes=[btB[SQ1]])
            yield
            P.op(P.pe, mm_chain(U[ussi][:, :Tn], [(onesb[:], bt[SQ0][:, :Tn]), (onesb[:], bt[SQ1][:, :Tn])]),
                 reads=[btB[SQ0], btB[SQ1], constB], writes=[UB[ussi]])
            yield
            rstd_from(U[ussi][:, :Tn], [UB[ussi]], 256, ft[RS][:, :Tn], ftB[RS])
            yield
            for j, ps in enumerate((ps0, ps1)):
                P.op(P.dve, (lambda j, ps: lambda e: e.scalar_tensor_tensor(out=out_t[:, j, :Tn], in0=ps, scalar=cf[:, gcol + j:gcol + j + 1],
                                                                              in1=ft[RS][:, :Tn], op0=ALU.mult, op1=ALU.mult))(j, ps),
                     reads=[psB_[j], ftB[RS], constB], writes=[out_B])
                yield

        def mla_latents(ntl, meta, rpar, blk0):
            Tn = ntl * 128
            w, wB = wget(("L1", 0))
            wv = w[:].rearrange("p (k c) -> p k c", c=512)
            uTr = uTB[:ntl]
            for i in range(4):
                if meta and i < 2:
                    continue
                P.op(P.pe, mm_chain(U[i][:, :Tn], [(wv[:, k, i * 128:(i + 1) * 128], uT[:, k, :Tn]) for k in range(8)]),
                     reads=[wB] + uTr, writes=[UB[i]])
                yield
            if not meta:
                yield from latent_norm(U[0][:, :Tn], U[1][:, :Tn], [UB[0], UB[1]], 8, 33, cqn, cqnB, Tn)
            yield from latent_norm(U[2][:, :Tn], U[3][:, :Tn], [UB[2], UB[3]], 9, 35, ckvn, ckvnB, Tn)
            w2, w2B = wget(("L2", 0))
            w2v = w2[:, 0:8 * 128].rearrange("p (k c) -> p k c", c=128)
            P.op(P.pe, mm_chain(U[10][0:64, :Tn], [(w2v[:, k, 0:64], uT[:, k, :Tn]) for k in range(8)]),
                 reads=[w2B] + uTr, writes=[UB[10]])
            yield
            P.op(P.pe, mm_chain(U[11][0:64, :Tn], [(w2v[:, k, 64:128], uT[:, k, :Tn]) for k in range(8)]),
                 reads=[w2B] + uTr, writes=[UB[11]])
            yield
            rope_apply(U[10][0:64, :Tn], U[11][0:64, :Tn], [UB[10], UB[11]], rpar, Tn,
                       Kpe[0:64, blk0 * 128: blk0 * 128 + Tn], [KpeB[blk0 + t] for t in range(ntl)], fa=0)
            yield

        def kv_gen(ntl, blk0):
            Tn = ntl * 128
            w, wB = wget(("WKV", 0))
            wv = w[:].rearrange("p (k c) -> p k c", c=2048)
            for h in range(NH):
                u = (8, 10, 12)[h % 3]
                P.op(P.pe, mm_chain(U[u][:, :Tn], [(wv[:, j, h * 128:(h + 1) * 128], ckvn[:, j, :Tn]) for j in range(2)]),
                     reads=[wB, ckvnB], writes=[UB[u]])
                eng = P.act if h % 2 == 0 else P.dve
                if h % 2 == 0:
                    fn = (lambda h, u: lambda e: e.activation(out=Kn[:, h, blk0 * 128: blk0 * 128 + Tn], in_=U[u][:, :Tn], func=AF.Copy))(h, u)
                else:
                    fn = (lambda h, u: lambda e: e.tensor_copy(out=Kn[:, h, blk0 * 128: blk0 * 128 + Tn], in_=U[u][:, :Tn]))(h, u)
                P.op(eng, fn, reads=[UB[u]], writes=[KnB[blk0 + t] for t in range(ntl)])
            i = 0
            for t in range(ntl):
                for hf in range(2):
                    bank = i % 2
                    i += 1
                    full = psF[bank][:, :]
                    P.op(P.pe, mm_chain(full, [(ckvn[:, j, t * 128:(t + 1) * 128], wv[:, j, 1024 + hf * 512: 1024 + (hf + 1) * 512])
                                               for j in range(2)]),
                         reads=[wB, ckvnB], writes=[UB[2 * bank], UB[2 * bank + 1]])
                    if hf == 0:
                        fn = (lambda t, hf, full: lambda e: e.activation(out=Vc[:, blk0 + t, hf * 512:(hf + 1) * 512], in_=full, func=AF.Copy))(t, hf, full)
                        P.op(P.act, fn, reads=[UB[2 * bank], UB[2 * bank + 1]], writes=[VB[blk0 + t]])
                    else:
                        fn = (lambda t, hf, full: lambda e: e.tensor_copy(out=Vc[:, blk0 + t, hf * 512:(hf + 1) * 512], in_=full))(t, hf, full)
                        P.op(P.dve, fn, reads=[UB[2 * bank], UB[2 * bank + 1]], writes=[VB[blk0 + t]])

        def attention(j, rpar, blk0):
            w, wB = wget(("WQ", 0))
            wv = w[:].rearrange("p (k c) -> p k c", c=2048)
            nkb = blk0 + NT
            for h in range(NH):
                un, up_, us_ = (8, 10, 11) if h % 2 == 0 else (4, 6, 7)
                P.op(P.pe, mm_chain(U[un][:, :], [(wv[:, jj, h * 256: h * 256 + 128], cqn[:, jj, :]) for jj in range(2)]),
                     reads=[wB, cqnB], writes=[UB[un]])
                P.op(P.pe, mm_chain(U[up_][0:64, :], [(wv[:, jj, h * 256 + 128: h * 256 + 192], cqn[:, jj, :]) for jj in range(2)]),
                     reads=[wB, cqnB], writes=[UB[up_]])
                P.op(P.pe, mm_chain(U[us_][0:64, :], [(wv[:, jj, h * 256 + 192: h * 256 + 256], cqn[:, jj, :]) for jj in range(2)]),
                     reads=[wB, cqnB], writes=[UB[us_]])
                P.op(P.act, (lambda un, h: lambda e: e.activation(out=qn[:, h, :], in_=U[un][:, :], func=AF.Copy))(un, h),
                     reads=[UB[un]], writes=[qnB[h]])
                rope_apply(U[up_][0:64, :], U[us_][0:64, :], [UB[up_], UB[us_]], rpar, T, qpe[0:64, h, :], [qpeB[h]], fa=8 + h % 2 * 2)
            rr = {"pt0": 0, "pt1": 0}
            c0_of = lambda kb: (kb - blk0) * 128 if kb >= blk0 else 0
            groups = []
            kb = 0
            while kb < nkb:
                if kb + 1 < nkb and c0_of(kb) == 0 and c0_of(kb + 1) == 0:
                    groups.append([kb, kb + 1])
                    kb += 2
                else:
                    groups.append([kb])
                    kb += 1

            def head_gen(h, uo, ud, sbanks):
                hp = h % 2

                def s_op(gi):
                    bank = sbanks[gi % 2]
                    blks = groups[gi]

                    def fn(e):
                        ins = None
                        for i, kb in enumerate(blks):
                            c0 = c0_of(kb)
                            dst = psF[bank][:, i * T + c0:(i + 1) * T]
                            e.matmul(dst, lhsT=Kn[:, h, kb * 128:(kb + 1) * 128], rhs=qn[:, h, c0:], start=True, stop=False)
                            ins = e.matmul(dst, lhsT=Kpe[:, kb * 128:(kb + 1) * 128], rhs=qpe[:, h, c0:], start=False, stop=True)
                        return ins
                    P.op(P.pe, fn, reads=[KnB[kb] for kb in blks] + [KpeB[kb] for kb in blks] + [qnB[h], qpeB[h]], writes=[bankB[bank]])

                s_op(0)
                yield
                for gi, blks in enumerate(groups):
                    if gi + 1 < len(groups):
                        s_op(gi + 1)
                        yield
                    bank = sbanks[gi % 2]
                    pi = 2 * hp + rr["pt" + str(hp)] % 2
                    rr["pt" + str(hp)] += 1
                    lo = c0_of(blks[0])
                    hi_ = len(blks) * T
                    P.op(P.act, (lambda bank, pi, lo, hi_: lambda e: e.activation(out=PT[pi][:, lo:hi_], in_=psF[bank][:, lo:hi_], func=AF.Exp, scale=SCALE))(bank, pi, lo, hi_),
                         reads=[bankB[bank]], writes=[PTB[pi]])
                    yield
                    for i, kb in enumerate(blks):
                        if kb >= blk0:
                            o0 = i * T + c0_of(kb)
                            P.op(P.dve, (lambda o0, pi: lambda e: e.tensor_tensor(out=PT[pi][:, o0:o0 + 128], in0=PT[pi][:, o0:o0 + 128],
                                                                                   in1=triu, op=ALU.mult))(o0, pi),
                                 reads=[PTB[pi], constB], writes=[PTB[pi]])
                            yield

                    def pv_fn(e, blks=blks, pi=pi):
                        ins = None
                        for i, kb in enumerate(blks):
                            c0 = c0_of(kb)
                            first, last = kb == 0, kb == nkb - 1
                            onesl = ones_meta if kb == 0 else onesb[:]
                            rhs = PT[pi][:, i * T + c0:(i + 1) * T]
                            e.matmul(U[uo][:, c0:], lhsT=Vc[:, kb, h * 128:(h + 1) * 128], rhs=rhs, start=first, stop=last)
                            ins = e.matmul(U[ud][:, c0:], lhsT=onesl, rhs=rhs, start=first, stop=last)
                        return ins
                    P.op(P.pe, pv_fn, reads=[VB[kb] for kb in blks] + [PTB[pi], constB], writes=[UB[uo], UB[ud]])
                    yield
                RD = 6 + hp
                P.op(P.dve, lambda e: e.reciprocal(out=ft[RD][:, :], in_=U[ud][:, :]), reads=[UB[ud]], writes=[ftB[RD]])
                yield
                P.op(P.dve, lambda e: e.tensor_tensor(out=big[:, 8 + h, :], in0=U[uo][:, :], in1=ft[RD][:, :], op=ALU.mult),
                     reads=[UB[uo], ftB[RD]], writes=[bigB[8 + h]])
                yield

            for h in range(0, NH, 2):
                run_interleaved([head_gen(h, 4, 6, (0, 1)), head_gen(h + 1, 8, 10, (6, 7))])

        def mix_stage():
            for fc in range(8):
                w, wB = wget(("MX", fc))
                wv = w[:].rearrange("p (k c) -> p k c", c=512)
                b0 = 0 if fc % 2 == 0 else 4
                pa, pb_, pga, pgb = b0, b0 + 1, b0 + 2, b0 + 3
                P.op(P.pe, mm_chain(U[pa][:, :], [(wv[:, k, 0:128], big[:, k, :]) for k in range(8)]),
                     reads=[wB] + bigB[0:8], writes=[UB[pa]])
                P.op(P.pe, mm_chain(U[pb_][:, :], [(wv[:, k, 128:256], big[:, 8 + k, :]) for k in range(8)]),
                     reads=[wB] + bigB[8:16], writes=[UB[pb_]])
                P.op(P.pe, mm_chain(U[pga][:, :], [(wv[:, k, 256:384], uT[:, k, :]) for k in range(8)]),
                     reads=[wB] + uTB, writes=[UB[pga]])
                P.op(P.pe, mm_chain(U[pgb][:, :], [(wv[:, k, 384:512], uT[:, k, :]) for k in range(8)]),
                     reads=[wB] + uTB, writes=[UB[pgb]])
                GA, GB, TA, TB = 0, 1, 2, 3
                P.op(P.act, (lambda fc, pga: lambda e: e.activation(out=ft[GA][:, :], in_=U[pga][:, :], func=AF.Sigmoid, bias=cf[:, 37 + fc:38 + fc]))(fc, pga),
                     reads=[UB[pga], constB], writes=[ftB[GA]])
                P.op(P.act, (lambda fc, pgb: lambda e: e.activation(out=ft[GB][:, :], in_=U[pgb][:, :], func=AF.Sigmoid, bias=cf[:, 45 + fc:46 + fc]))(fc, pgb),
                     reads=[UB[pgb], constB], writes=[ftB[GB]])
                P.op(P.dve, (lambda pa: lambda e: e.tensor_tensor(out=ft[TA][:, :], in0=U[pa][:, :], in1=ft[GA][:, :], op=ALU.mult))(pa),
                     reads=[UB[pa], ftB[GA]], writes=[ftB[TA]])
                P.op(P.dve, (lambda pb_: lambda e: e.tensor_tensor(out=ft[TB][:, :], in0=U[pb_][:, :], in1=ft[GB][:, :], op=ALU.mult))(pb_),
                     reads=[UB[pb_], ftB[GB]], writes=[ftB[TB]])
                P.op(P.pool, (lambda fc: lambda e: e.tensor_tensor(out=big[:, 16 + fc, :], in0=ft[TA][:, :], in1=ft[TB][:, :], op=ALU.add))(fc),
                     reads=[ftB[TA], ftB[TB]], writes=[bigB[16 + fc]])

        def out_proj(par, chunks, in_off, gcol, final_store, seq_i, tok0, after_tile=None):
            accs = [(psF[4 + t][:, :], [UB[8 + 2 * t], UB[9 + 2 * t]]) for t in range(NT)]
            ssq = [[sm(f"op_ss{t}_{hf}") for hf in range(2)] for t in range(NT)]
            for hf in range(2):
                ngroups = len(chunks[hf])
                for gi, (cname, k0, k1) in enumerate(chunks[hf]):
                    w, wB = wget(cname, hold=True)
                    kk = state["lastk"]
                    wv = w[:, 0:(k1 - k0) * 512].rearrange("p (k c) -> p k c", c=512)
                    for t in range(NT):
                        acc, accB = accs[t]

                        def fn(e, t=t, acc=acc, k0=k0, k1=k1, wv=wv, gi=gi, ngroups=ngroups):
                            ins = None
                            for k in range(k0, k1):
                                ins = e.matmul(acc, lhsT=big[:, in_off + k, t * 128:(t + 1) * 128], rhs=wv[:, k - k0, :],
                                               start=(gi == 0 and k == k0), stop=(gi == ngroups - 1 and k == k1 - 1))
                            return ins
                        P.op(P.pe, fn, reads=[wB] + bigB[in_off + k0: in_off + k1], writes=accB)
                        if t == NT - 1:
                            release(kk)
                        yield
                for t in range(NT):
                    acc, accB = accs[t]
                    ss_ap, ssB = ssq[t][hf]
                    P.op(P.act, (lambda acc, ss_ap, t, hf: lambda e: e.activation(out=xn2[:, t, hf * 512:(hf + 1) * 512], in_=acc, func=AF.Square, accum_out=ss_ap))(acc, ss_ap, t, hf),
                         reads=accB, writes=[xn2B[t], ssB])
                    yield
                    if hf == 0:
                        P.op(P.dve, (lambda t, acc: lambda e: e.tensor_copy(out=y0[:, t, :], in_=acc))(t, acc), reads=accB, writes=[y0B[t]])
                        yield
            def epi(t):
                acc, accB = accs[t]
                (s0, s0B), (s1, s1B) = ssq[t]
                rs_ap, rsB = sm(f"op_rs{t}")
                P.op(P.dve, lambda e: e.tensor_tensor(out=s0, in0=s0, in1=s1, op=ALU.add), reads=[s0B, s1B], writes=[s0B])
                yield
                rstd_from(s0, [s0B], D, rs_ap, rsB)
                yield
                for hf in range(2):
                    src = y0[:, t, :] if hf == 0 else acc
                    srcB = [y0B[t]] if hf == 0 else accB
                    P.op(P.dve, (lambda src, hf: lambda e: e.scalar_tensor_tensor(out=src, in0=src, scalar=rs_ap,
                                                                                  in1=gpostb[:, gcol + hf * 512: gcol + (hf + 1) * 512],
                                                                                  op0=ALU.mult, op1=ALU.mult))(src, hf),
                         reads=srcB + [rsB, constB], writes=srcB)
                    yield
                    P.op(P.dve, (lambda src, hf: lambda e: e.tensor_tensor(out=hb[:, par, t, hf * 512:(hf + 1) * 512],
                                                                           in0=hb[:, par, t, hf * 512:(hf + 1) * 512], in1=src, op=ALU.add))(src, hf),
                         reads=srcB + [hB[par][t]], writes=[hB[par][t]])
                    yield
                if after_tile is not None:
                    yield from after_tile(t)
                if final_store:
                    P.op(P.pool, lambda e: e.dma_start(out=outd[seq_i, tok0 + t * 128: tok0 + (t + 1) * 128, :], in_=hb[:, par, t, :]),
                         reads=[hB[par][t]], writes=[], dsem=stS[par][t])
                    yield
            yield from rr_merge([epi(t) for t in range(NT)])

        def ffn_in(mid=None):
            for jx in range(11):
                if jx == 7 and mid is not None:
                    mid()
                w, wB = wget(("FI", jx))
                wv = w[:].rearrange("p (k c) -> p k c", c=512)
                for sub in range(2):
                    hc = 2 * jx + sub
                    b0 = ((2 * jx + sub) % 4) * 2
                    pg, pu = b0, b0 + 1
                    P.op(P.pe, mm_chain(U[pg][:, :], [(wv[:, k, sub * 256: sub * 256 + 128], uT[:, k, :]) for k in range(8)]),
                         reads=[wB] + uTB, writes=[UB[pg]])
                    P.op(P.pe, mm_chain(U[pu][:, :], [(wv[:, k, sub * 256 + 128: sub * 256 + 256], uT[:, k, :]) for k in range(8)]),
                         reads=[wB] + uTB, writes=[UB[pu]])
                    si = hc % 2
                    P.op(P.act, (lambda pg, si: lambda e: e.activation(out=ft[si][:, :], in_=U[pg][:, :], func=AF.Silu))(pg, si),
                         reads=[UB[pg]], writes=[ftB[si]])
                    P.op(P.dve, (lambda pu, si, hc: lambda e: e.tensor_tensor(out=big[:, hc, :], in0=U[pu][:, :], in1=ft[si][:, :], op=ALU.mult))(pu, si, hc),
                         reads=[UB[pu], ftB[si]], writes=[bigB[hc]])

        def load_x(par, seq_i, j, meta):
            if meta:
                P.op(P.sp, lambda e: e.dma_start(out=hb[:, par, 0, :], in_=xmeta), writes=[hB[par][0]], dsem=xS[par])
                P.op(P.sp, lambda e: e.dma_start(out=ropeb[:, par, :, 0:128], in_=roped[:, :, 0:128]), writes=[ropeB[par]], dsem=rS[par])
            else:
                src = xs[seq_i, j * T:(j + 1) * T, :].rearrange("(t p) d -> p t d", p=128)
                P.op(P.sp, lambda e: e.dma_start(out=hb[:, par, :, :], in_=src), writes=hB[par], dsem=xS[par])
                c0 = 128 + j * T
                P.op(P.sp, lambda e: e.dma_start(out=ropeb[:, par, :, :], in_=roped[:, :, c0:c0 + T]), writes=[ropeB[par]], dsem=rS[par])

        def run_interleaved(gens):
            gens = list(gens)
            while gens:
                for g in list(gens):
                    try:
                        next(g)
                    except StopIteration:
                        gens.remove(g)

        work = [("meta", 0, 0)] + [("tile", s, j) for s in range(nseq) for j in range(nsup)]
        if use_order is not None:
            use_seq.extend(use_order)
        load_x(0, 0, 0, True)
        hi = 0
        prepped = {}
        pre_done = {}
        for wi, (kind, s, j) in enumerate(work):
            par = wi % 2
            meta = kind == "meta"
            ntl = 1 if meta else NT
            blk0 = 0 if meta else 1 + j * NT
            if (not meta) and j == 0:
                P.op(P.pool, lambda e: e.tensor_copy(out=S[:], in_=Smeta[:]), reads=[SmetaB], writes=SB)
            uT, uTB = uTm, uTmB
            if not prepped.get(wi, False):
                run_interleaved([chain(norm_stats(par, t), norm_tr(t, 0, uTm, uTmB)) for t in range(ntl)])
            if wi in pre_done:
                cxs = pre_done[wi]
                run_interleaved([hgrn_A(1, ntl, meta, hi, cxs[1])])
                hi += 1
            else:
                cxs = [dict() for _ in range(NH)]
                run_interleaved([hgrn_A(0, ntl, meta, hi, cxs[0])])
                hi += 1
                run_interleaved([hgrn_B(cxs[0]), hgrn_A(1, ntl, meta, hi, cxs[1])])
                hi += 1
            for h in range(NH):
                a_gen = None
                if h + 2 < NH:
                    a_gen = hgrn_A(h + 2, ntl, meta, hi, cxs[h + 2])
                    hi += 1
                gens = [hgrn_C(cxs[h], a_gen)]
                if h + 1 < NH:
                    gens.insert(0, hgrn_B(cxs[h + 1]))
                else:
                    if wi + 1 < len(work):
                        k2, s2, j2 = work[wi + 1]
                        load_x((wi + 1) % 2, s2, j2, k2 == "meta")
                    gens.append(mla_latents(ntl, meta, par, blk0))
                run_interleaved(gens)
            kv_gen(ntl, blk0)
            P.op(P.pool, lambda e: e.memset(dummy[:], 0.0),
                 writes=[b for t in range(ntl) for b in (KnB[blk0 + t], KpeB[blk0 + t], VB[blk0 + t])])
            if meta:
                P.op(P.pool, lambda e: e.tensor_copy(out=Smeta[:], in_=S[:]), reads=SB, writes=[SmetaB])
                continue
            attention(j, par, blk0)
            mix_stage()
            run_interleaved([out_proj(par, [[(("WO", 0), 0, 8)], [(("WO", 1), 0, 8)]], 16, 0, False, s, j * T,
                                      after_tile=lambda t: chain(norm_stats(par, t), norm_tr(t, 8, uTf, uTfB)))])
            uT, uTB = uTf, uTfB
            nxt = wi + 1 < len(work) and work[wi + 1][0] == "tile"
            ffn_in(mid=(lambda: run_interleaved([norm_stats((wi + 1) % 2, t) for t in range(NT)])) if nxt else None)
            if nxt:
                run_interleaved([norm_tr(t, 0, uTm, uTmB) for t in range(NT)])
                prepped[wi + 1] = True
            fo = out_proj(par, [[(("FO", hf * 3 + g), k0, k1) for g, (k0, k1) in enumerate(FO_GROUPS)] for hf in range(2)],
                          0, D, True, s, j * T)
            if nxt:
                uT, uTB = uTm, uTmB
                ncx = [dict() for _ in range(NH)]

                def pre(hi0=hi, ncx=ncx):
                    yield from hgrn_A(0, NT, False, hi0, ncx[0])
                    yield from hgrn_B(ncx[0])
                hi += 1
                run_interleaved([fo, pre()])
                pre_done[wi + 1] = ncx
            else:
                run_interleaved([fo])
        if use_order is None:
            return recorded
        assert state["use"] == len(use_seq) and state["load"] == len(use_seq) and not held
        P.wait_only(P.pool, [(d.sem, d.val) for row in stS for d in row if d.val > 0])

        with nc.Block() as block:
            @block.tensor
            def _(e):
                _emit(P.pe, e)

            @block.scalar
            def _(e):
                _emit(P.act, e)

            @block.vector
            def _(e):
                _emit(P.dve, e)

            @block.gpsimd
            def _(e):
                _emit(P.pool, e)

            @block.sync
            def _(e):
                _emit(P.sp, e)
    return nc


def _host_inputs(x, meta_tokens, w_in, b_gate, lb_logits, hg_norm_g, w_hg_o, q_a_norm_g, w_q_b, kv_a_norm_g,
                 w_kv_b, w_mla_o, w_out, mix_pre_g, mix_post_g, ffn_pre_g, ffn_post_g, w_ffn_in, w_ffn_out):
    f = lambda a: np.ascontiguousarray(np.asarray(a, dtype=np.float32))
    wp = _pack_weights(f(w_in)[0], f(w_hg_o)[0], f(w_q_b)[0], f(w_kv_b)[0], f(w_mla_o)[0], f(w_out)[0],
                       f(w_ffn_in)[0], f(w_ffn_out)[0])
    cfm = np.zeros((128, NCF), np.float32)
    fm = lambda v: f(v).reshape(-1, 128).T
    cfm[:, 0:8] = fm(mix_pre_g[0])
    cfm[:, 8:16] = fm(ffn_pre_g[0])
    cfm[:, 16:24] = fm(lb_logits[0])
    cfm[:, 24:32] = fm(lb_logits[1])
    cfm[:, 32:33] = fm(hg_norm_g[0])
    cfm[:, 33:35] = fm(q_a_norm_g[0])
    cfm[:, 35:37] = fm(kv_a_norm_g[0])
    cfm[:, 37:53] = fm(b_gate[0])
    cfm[112:, 53] = 1.0
    gp = np.concatenate([f(mix_post_g[0]), f(ffn_post_g[0])])[None, :].repeat(128, axis=0)
    xm = np.zeros((128, D), np.float32)
    xm[112:] = f(meta_tokens)
    common = {"xmeta": xm, "wpack": wp, "cf32": cfm, "gpost": np.ascontiguousarray(gp),
              "rope": _rope_tables(), "cmat": _const_mats()}
    xf = f(x)
    return [dict(common, xs=np.ascontiguousarray(xf[c * NSEQ:(c + 1) * NSEQ])) for c in range(NCORES)]


def kernel(**inputs):
    in_maps = _host_inputs(**inputs)
    nc = build_program()
    res = run_bass_kernel_spmd(nc, in_maps, core_ids=list(range(NCORES)))
    return np.concatenate([np.asarray(r["out"], dtype=np.float32) for r in res.results], axis=0)
```
